# Optimizing a Trainium2 kernel written in Bass

```python
import math
import jax, jax.numpy as jnp
from jax import lax
import numpy as np

D_MODEL = 1024
BATCH = 2
SEQ = 8192
DEPTH = 1

NSA_HEAD_DIM = 64
NSA_HEADS = D_MODEL // NSA_HEAD_DIM
NSA_KV_HEADS = 4
CMP_BLK = 32
CMP_STRIDE = 16
CMP_HIDDEN = 256
SEL_BLK = 64
N_SEL = 16
WINDOW = 512
NSA_Q_BLK = 64

GDN_HEAD_DIM = 128
GDN_HEADS = D_MODEL // GDN_HEAD_DIM
GDN_CONV = 4
GDN_CHUNK = 64

FFN_DIM = 2816
FFN_CONV = 3

NORM_EPS = 1e-6
MASK_VALUE = -1e30
FORCED_SCORE = 1e6

NSA_Q_COLS = NSA_HEADS * NSA_HEAD_DIM
NSA_KV_COLS = 6 * NSA_KV_HEADS * NSA_HEAD_DIM
NSA_GATE_COLS = 3 * NSA_HEADS
GDN_QKV_COLS = 3 * GDN_HEADS * GDN_HEAD_DIM
GDN_Z_COLS = GDN_HEADS * GDN_HEAD_DIM
MERGE_COLS = 2 * D_MODEL
IN_SPLITS = (NSA_Q_COLS, NSA_KV_COLS, NSA_GATE_COLS, GDN_QKV_COLS, GDN_HEADS, GDN_HEADS, GDN_Z_COLS, MERGE_COLS)
IN_COLS = NSA_Q_COLS + NSA_KV_COLS + NSA_GATE_COLS + GDN_QKV_COLS + 2 * GDN_HEADS + GDN_Z_COLS + MERGE_COLS

kernel_name = 'hybrid_nsa_gdn_convffn_adaln'


def rms_norm(x, w):
    xf = x.astype(jnp.float32)
    y = xf * lax.rsqrt(jnp.mean(xf * xf, axis=-1, keepdims=True) + NORM_EPS)
    return (y * w.astype(jnp.float32)).astype(x.dtype)


def l2_normalize(x):
    xf = x.astype(jnp.float32)
    return xf * lax.rsqrt(jnp.sum(xf * xf, axis=-1, keepdims=True) + NORM_EPS)


def masked_softmax(s, mask):
    s = jnp.where(mask, s.astype(jnp.float32), MASK_VALUE)
    return jnp.where(mask, jax.nn.softmax(s, axis=-1), 0.0)


def causal_depthwise_conv(x, w):
    width, ch = w.shape
    return lax.conv_general_dilated(
        x, w.astype(x.dtype)[:, None, :], window_strides=(1,),
        padding=((width - 1, 0),), dimension_numbers=('NWC', 'WIO', 'NWC'),
        feature_group_count=ch)


def nsa_attention(q, k_c, v_c, k_s, v_s, k_w, v_w, gate_logits, pos_k, pos_v,
                  ck_w1, ck_b1, ck_w2, ck_b2, cv_w1, cv_b1, cv_w2, cv_b2):
    bsz, seq, n_heads, dk = q.shape
    n_grp = k_c.shape[2]
    rep = n_heads // n_grp
    n_cmp = (seq - CMP_BLK) // CMP_STRIDE + 1
    n_blk = seq // SEL_BLK
    n_sel = min(N_SEL, n_blk)
    scale = dk ** -0.5

    c_start = np.arange(n_cmp) * CMP_STRIDE
    cmp_idx = c_start[:, None] + np.arange(CMP_BLK)[None, :]
    s_start = np.arange(n_blk) * SEL_BLK
    overlap = np.clip(np.minimum(c_start[:, None] + CMP_BLK, s_start[None, :] + SEL_BLK)
                      - np.maximum(c_start[:, None], s_start[None, :]), 0, None) / CMP_BLK
    overlap = jnp.asarray(overlap, jnp.float32)
    c_end = jnp.asarray(c_start + CMP_BLK - 1)

    def compress(t, pos, w1, b1, w2, b2):
        blk = t[:, cmp_idx] + pos[None, None, :, None, :]
        blk = blk.transpose(0, 3, 1, 2, 4).reshape(bsz, n_grp, n_cmp, CMP_BLK * dk)
        return jax.nn.gelu(blk @ w1 + b1) @ w2 + b2

    kc = compress(k_c, pos_k, ck_w1, ck_b1, ck_w2, ck_b2)
    vc = compress(v_c, pos_v, cv_w1, cv_b1, cv_w2, cv_b2)
    ks_blk = k_s.reshape(bsz, n_blk, SEL_BLK, n_grp, dk).transpose(0, 3, 1, 2, 4)
    vs_blk = v_s.reshape(bsz, n_blk, SEL_BLK, n_grp, dk).transpose(0, 3, 1, 2, 4)
    pad = ((0, 0), (0, 0), (WINDOW, 0), (0, 0))
    kw_pad = jnp.pad(k_w.transpose(0, 2, 1, 3), pad)
    vw_pad = jnp.pad(v_w.transpose(0, 2, 1, 3), pad)
    qg = q.reshape(bsz, seq, n_grp, rep, dk).transpose(0, 2, 3, 1, 4)
    gates = jax.nn.sigmoid(gate_logits.astype(jnp.float32)).astype(q.dtype)
    gates = gates.reshape(bsz, seq, n_grp, rep, 3).transpose(0, 2, 3, 1, 4)
    b_ix = jnp.arange(bsz)[:, None, None, None]
    g_ix = jnp.arange(n_grp)[None, :, None, None]
    j_blk = jnp.arange(n_blk)
    w_off = jnp.arange(WINDOW + NSA_Q_BLK)

    def block(qb):
        qs = qb * NSA_Q_BLK
        t = qs + jnp.arange(NSA_Q_BLK)
        qi = lax.dynamic_slice_in_dim(qg, qs, NSA_Q_BLK, axis=3)
        gi = lax.dynamic_slice_in_dim(gates, qs, NSA_Q_BLK, axis=3)
        s_c = jnp.einsum('bgrqd,bgnd->bgrqn', qi, kc).astype(jnp.float32) * scale
        p_c = masked_softmax(s_c, c_end[None, :] <= t[:, None])
        o_c = jnp.einsum('bgrqn,bgnd->bgrqd', p_c.astype(vc.dtype), vc)
        imp = jnp.einsum('bgrqn,nj->bgqj', p_c, overlap)
        cur = (t // SEL_BLK)[:, None]
        forced = (j_blk == 0) | (j_blk == cur) | (j_blk == cur - 1)
        valid = j_blk * SEL_BLK <= t[:, None]
        score = jnp.where(forced, FORCED_SCORE, jnp.where(valid, imp, -1.0))
        top_v, top_i = lax.top_k(score, n_sel)
        ks = ks_blk[b_ix, g_ix, top_i]
        vs = vs_blk[b_ix, g_ix, top_i]
        kpos = top_i[..., None] * SEL_BLK + jnp.arange(SEL_BLK)
        m_s = (top_v >= 0.0)[..., None] & (kpos <= t[:, None, None])
        m_s = m_s.reshape(bsz, n_grp, 1, NSA_Q_BLK, n_sel * SEL_BLK)
        s_s = jnp.einsum('bgrqd,bgqnkd->bgrqnk', qi, ks).astype(jnp.float32) * scale
        p_s = masked_softmax(s_s.reshape(bsz, n_grp, rep, NSA_Q_BLK, n_sel * SEL_BLK), m_s)
        p_s = p_s.reshape(bsz, n_grp, rep, NSA_Q_BLK, n_sel, SEL_BLK)
        o_s = jnp.einsum('bgrqnk,bgqnkd->bgrqd', p_s.astype(vs.dtype), vs)
        kw = lax.dynamic_slice_in_dim(kw_pad, qs, WINDOW + NSA_Q_BLK, axis=2)
        vw = lax.dynamic_slice_in_dim(vw_pad, qs, WINDOW + NSA_Q_BLK, axis=2)
        wpos = (qs - WINDOW + w_off)[None, :]
        m_w = (wpos <= t[:, None]) & (wpos > t[:, None] - WINDOW) & (wpos >= 0)
        s_w = jnp.einsum('bgrqd,bgkd->bgrqk', qi, kw).astype(jnp.float32) * scale
        p_w = masked_softmax(s_w, m_w)
        o_w = jnp.einsum('bgrqk,bgkd->bgrqd', p_w.astype(vw.dtype), vw)
        return gi[..., 0:1] * o_c + gi[..., 1:2] * o_s + gi[..., 2:3] * o_w

    out = lax.map(block, jnp.arange(seq // NSA_Q_BLK))
    return out.transpose(1, 0, 4, 2, 3, 5).reshape(bsz, seq, n_heads * dk)


def chunk_gated_delta_rule(q, k, v, g, beta):
    bsz, seq, n_heads, dk = q.shape
    dv = v.shape[-1]
    cs = GDN_CHUNK
    n_chunk = seq // cs
    f32 = jnp.float32

    def chunks(t):
        t = t.astype(f32).reshape((bsz, n_chunk, cs, n_heads) + t.shape[3:])
        return jnp.swapaxes(t, 2, 3)

    q = chunks(q) * (dk ** -0.5)
    k = chunks(k)
    v = chunks(v)
    g = chunks(g)
    beta = chunks(beta)
    gc = jnp.cumsum(g, axis=-1)
    incl = jnp.tril(jnp.ones((cs, cs), bool))
    strict = jnp.tril(jnp.ones((cs, cs), bool), -1)
    decay = jnp.exp(jnp.where(incl, gc[..., :, None] - gc[..., None, :], -jnp.inf))
    kk = jnp.einsum('bnhid,bnhjd->bnhij', k, k)
    a_mat = jnp.eye(cs, dtype=f32) + jnp.where(strict, beta[..., :, None] * kk * decay, 0.0)
    rhs = jnp.concatenate([v * beta[..., None], k * (beta * jnp.exp(gc))[..., None]], axis=-1)
    sol = lax.linalg.triangular_solve(a_mat, rhs, left_side=True, lower=True, unit_diagonal=True)
    u, w = sol[..., :dv], sol[..., dv:]
    qk = jnp.einsum('bnhid,bnhjd->bnhij', q, k) * decay
    q_dec = q * jnp.exp(gc)[..., None]
    k_dec = k * jnp.exp(gc[..., -1:] - gc)[..., None]
    g_tot = jnp.exp(gc[..., -1])

    def step(state, xs):
        u_i, w_i, qk_i, qd_i, kd_i, gt_i = xs
        v_new = u_i - jnp.einsum('bhck,bhkv->bhcv', w_i, state)
        o_i = jnp.einsum('bhck,bhkv->bhcv', qd_i, state) + jnp.einsum('bhij,bhjv->bhiv', qk_i, v_new)
        state = state * gt_i[..., None, None] + jnp.einsum('bhck,bhcv->bhkv', kd_i, v_new)
        return state, o_i

    xs = tuple(jnp.moveaxis(t, 1, 0) for t in (u, w, qk, q_dec, k_dec, g_tot))
    state0 = jnp.zeros((bsz, n_heads, dk, dv), f32)
    _, o = lax.scan(step, state0, xs)
    return o.transpose(1, 0, 3, 2, 4).reshape(bsz, seq, n_heads, dv)


def gated_deltanet(qkv, a, b, z, conv_w, a_log, dt_bias, norm_w):
    bsz, seq, _ = qkv.shape
    qkv = jax.nn.silu(causal_depthwise_conv(qkv, conv_w))
    q, k, v = jnp.split(qkv, 3, axis=-1)
    shp = (bsz, seq, GDN_HEADS, GDN_HEAD_DIM)
    q = l2_normalize(q.reshape(shp))
    k = l2_normalize(k.reshape(shp))
    v = v.reshape(shp)
    g = -jnp.exp(a_log.astype(jnp.float32)) * jax.nn.softplus(a.astype(jnp.float32) + dt_bias.astype(jnp.float32))
    beta = jax.nn.sigmoid(b.astype(jnp.float32))
    o = chunk_gated_delta_rule(q, k, v, g, beta)
    o = rms_norm(o, norm_w) * jax.nn.silu(z.reshape(shp).astype(jnp.float32))
    return o.reshape(bsz, seq, GDN_HEADS * GDN_HEAD_DIM).astype(qkv.dtype)


def setup_inputs(seed: int = 0) -> dict:
    key = jax.random.key(seed)
    ks = iter(jax.random.split(key, 40))

    def nrm(shape, scale):
        return jax.random.normal(next(ks), shape, jnp.float32) * scale

    L, D, dk = DEPTH, D_MODEL, NSA_HEAD_DIM
    dt = jnp.exp(jax.random.uniform(next(ks), (L, GDN_HEADS), jnp.float32, math.log(1e-3), math.log(1e-1)))
    return {
        'x': nrm((BATCH, SEQ, D), 1.0),
        'c': nrm((BATCH, D), 1.0),
        'w_ada': nrm((L, D, 6 * D), 0.5 * D ** -0.5),
        'b_ada': nrm((L, 6 * D), 0.01),
        'norm_mix': 1.0 + nrm((L, D), 0.02),
        'w_in': nrm((L, D, IN_COLS), D ** -0.5),
        'nsa_pos_k': nrm((L, CMP_BLK, dk), 0.5),
        'nsa_pos_v': nrm((L, CMP_BLK, dk), 0.5),
        'nsa_ck_w1': nrm((L, CMP_BLK * dk, CMP_HIDDEN), (CMP_BLK * dk) ** -0.5),
        'nsa_ck_b1': nrm((L, CMP_HIDDEN), 0.01),
        'nsa_ck_w2': nrm((L, CMP_HIDDEN, dk), CMP_HIDDEN ** -0.5),
        'nsa_ck_b2': nrm((L, dk), 0.01),
        'nsa_cv_w1': nrm((L, CMP_BLK * dk, CMP_HIDDEN), (CMP_BLK * dk) ** -0.5),
        'nsa_cv_b1': nrm((L, CMP_HIDDEN), 0.01),
        'nsa_cv_w2': nrm((L, CMP_HIDDEN, dk), CMP_HIDDEN ** -0.5),
        'nsa_cv_b2': nrm((L, dk), 0.01),
        'gdn_conv': nrm((L, GDN_CONV, GDN_QKV_COLS), GDN_CONV ** -0.5),
        'gdn_a_log': jnp.log(jax.random.uniform(next(ks), (L, GDN_HEADS), jnp.float32, 1.0, 16.0)),
        'gdn_dt_bias': dt + jnp.log(-jnp.expm1(-dt)),
        'gdn_norm': 1.0 + nrm((L, GDN_HEAD_DIM), 0.02),
        'w_proj_nsa': nrm((L, NSA_Q_COLS, D), NSA_Q_COLS ** -0.5),
        'w_proj_gdn': nrm((L, GDN_Z_COLS, D), GDN_Z_COLS ** -0.5),
        'w_out': nrm((L, D, D), D ** -0.5),
        'norm_ffn': 1.0 + nrm((L, D), 0.02),
        'ffn_up': nrm((L, D, 2 * FFN_DIM), D ** -0.5),
        'ffn_conv': nrm((L, FFN_CONV, 2 * FFN_DIM), FFN_CONV ** -0.5),
        'ffn_conv_b': nrm((L, 2 * FFN_DIM), 0.01),
        'ffn_down': nrm((L, FFN_DIM, D), FFN_DIM ** -0.5),
        'norm_final': 1.0 + nrm((D,), 0.02),
    }


def reference(x, c, w_ada, b_ada, norm_mix, w_in, nsa_pos_k, nsa_pos_v,
              nsa_ck_w1, nsa_ck_b1, nsa_ck_w2, nsa_ck_b2,
              nsa_cv_w1, nsa_cv_b1, nsa_cv_w2, nsa_cv_b2,
              gdn_conv, gdn_a_log, gdn_dt_bias, gdn_norm,
              w_proj_nsa, w_proj_gdn, w_out, norm_ffn,
              ffn_up, ffn_conv, ffn_conv_b, ffn_down, norm_final):
    bsz, seq, _ = x.shape
    split_at = tuple(int(v) for v in np.cumsum(IN_SPLITS)[:-1])
    for l in range(DEPTH):
        mod = (c @ w_ada[l] + b_ada[l])[:, None, :]
        sh_m, sc_m, gt_m, sh_f, sc_f, gt_f = jnp.split(mod, 6, axis=-1)
        h = rms_norm(x, norm_mix[l]) * (1.0 + sc_m) + sh_m
        q_a, kv_a, gl_a, qkv_b, a_b, b_b, z_b, merge = jnp.split(h @ w_in[l], split_at, axis=-1)
        q_a = q_a.reshape(bsz, seq, NSA_HEADS, NSA_HEAD_DIM)
        kv_a = kv_a.reshape(bsz, seq, 6, NSA_KV_HEADS, NSA_HEAD_DIM)
        y_a = nsa_attention(q_a, kv_a[:, :, 0], kv_a[:, :, 1], kv_a[:, :, 2], kv_a[:, :, 3],
                            kv_a[:, :, 4], kv_a[:, :, 5], gl_a.reshape(bsz, seq, NSA_HEADS, 3),
                            nsa_pos_k[l], nsa_pos_v[l],
                            nsa_ck_w1[l], nsa_ck_b1[l], nsa_ck_w2[l], nsa_ck_b2[l],
                            nsa_cv_w1[l], nsa_cv_b1[l], nsa_cv_w2[l], nsa_cv_b2[l])
        y_b = gated_deltanet(qkv_b, a_b, b_b, z_b, gdn_conv[l], gdn_a_log[l], gdn_dt_bias[l], gdn_norm[l])
        g_a, g_b = jnp.split(jax.nn.sigmoid(merge.astype(jnp.float32)).astype(x.dtype), 2, axis=-1)
        mixed = g_a * (y_a @ w_proj_nsa[l]) + g_b * (y_b @ w_proj_gdn[l])
        x = x + gt_m * (mixed @ w_out[l])
        h = rms_norm(x, norm_ffn[l]) * (1.0 + sc_f) + sh_f
        u = causal_depthwise_conv(h @ ffn_up[l], ffn_conv[l]) + ffn_conv_b[l]
        u_gate, u_val = jnp.split(u, 2, axis=-1)
        x = x + gt_f * ((jax.nn.silu(u_gate) * u_val) @ ffn_down[l])
    return rms_norm(x, norm_final)
```

```python
import contextlib
import numpy as np
import ml_dtypes
import concourse.bass as bass
import concourse.mybir as mybir
from concourse.bass_utils import run_bass_kernel_spmd

F32 = mybir.dt.float32
BF16 = mybir.dt.bfloat16
I32 = mybir.dt.int32
AF = mybir.ActivationFunctionType
ALU = mybir.AluOpType
AX = mybir.AxisListType

D = 1024
NEG = -30000.0
EPS = 1e-6

ENGS = ("pe", "act", "dve", "pool", "sp")
DMA_POOL = 14


class Buf:
    __slots__ = ("name", "last_write", "readers", "excl")

    def __init__(self, name, excl=False):
        self.name = name
        self.last_write = None
        self.readers = []
        self.excl = excl


class Sched:
    def __init__(self, nc):
        self.nc = nc
        self.ops = []
        self.by_eng = {e: [] for e in ENGS}
        self.bar = set()
        self.since_bar = []

    def buf(self, name, excl=False):
        return Buf(name, excl)

    def op(self, eng, fn, reads=(), writes=(), ndma=0, cc=False):
        oid = len(self.ops)
        deps = set(self.bar)
        for b in reads:
            if b.excl:
                continue
            if b.last_write is not None:
                deps.add(b.last_write)
        wl = list(writes) + [b for b in reads if b.excl]
        for b in wl:
            if b.last_write is not None:
                deps.add(b.last_write)
            deps.update(b.readers)
        o = dict(id=oid, eng=eng, fn=fn, deps=deps, ndma=ndma, cc=cc)
        self.ops.append(o)
        self.by_eng[eng].append(o)
        self.since_bar.append(oid)
        for b in reads:
            if not b.excl:
                b.readers.append(oid)
        for b in wl:
            b.last_write = oid
            b.readers = []
        return oid

    def barrier(self):
        if not self.since_bar:
            return
        last = {}
        keep = set()
        for oid in self.since_bar:
            o = self.ops[oid]
            if o["ndma"] or o["cc"]:
                keep.add(oid)
            else:
                last[o["eng"]] = oid
        self.bar = set(last.values()) | keep
        self.since_bar = []

    def emit(self, final_wait_ops=()):
        nc = self.nc
        ops = self.ops
        for o in ops:
            o["signal"] = False
        for o in ops:
            for d in o["deps"]:
                if ops[d]["ndma"] == 0 and not ops[d]["cc"]:
                    ops[d]["signal"] = True
        for d in final_wait_ops:
            if ops[d]["ndma"] == 0 and not ops[d]["cc"]:
                ops[d]["signal"] = True
        cnt = {e: 0 for e in ENGS}
        dcnt = {e: 0 for e in ENGS}
        ncc = 0
        for e in ENGS:
            for o in self.by_eng[e]:
                if o["cc"]:
                    o["tok"] = ("x", ncc)
                    ncc += 1
                elif o["ndma"] == 0:
                    if o["signal"]:
                        cnt[e] += 1
                        o["tok"] = ("c", e, cnt[e])
                    else:
                        o["tok"] = None
                else:
                    o["dslot"] = dcnt[e] % DMA_POOL
                    dcnt[e] += 1
        with contextlib.ExitStack() as st:
            csem = {e: st.enter_context(nc.semaphore("c_" + e)) for e in ENGS}
            xsem = [st.enter_context(nc.semaphore("x_%d" % i)) for i in range(ncc)]
            dsem = {}
            for e in ("sp", "act", "pool"):
                if dcnt[e]:
                    dsem[e] = [st.enter_context(nc.semaphore("d_%s_%d" % (e, i)))
                               for i in range(min(DMA_POOL, dcnt[e]))]
            cum = {}
            for e in ENGS:
                for o in self.by_eng[e]:
                    if o["ndma"]:
                        key = (e, o["dslot"])
                        prev = cum.get(key, 0)
                        o["dprev"] = prev
                        cum[key] = prev + 16 * o["ndma"]
                        o["tok"] = ("d", e, o["dslot"], cum[key])
            block = st.enter_context(nc.Block())
            deco = {"pe": block.tensor, "act": block.scalar, "dve": block.vector,
                    "pool": block.gpsimd, "sp": block.sync}

            def make(e):
                def body(eng_h):
                    waited = {}

                    def wait_tok(t):
                        if t is None:
                            return
                        if t[0] == "c":
                            key, sem, val = ("c", t[1]), csem[t[1]], t[2]
                        elif t[0] == "d":
                            key, sem, val = ("d", t[1], t[2]), dsem[t[1]][t[2]], t[3]
                        else:
                            key, sem, val = ("x", t[1]), xsem[t[1]], 1
                        if waited.get(key, 0) >= val:
                            return
                        eng_h.wait_ge(sem, val)
                        waited[key] = val

                    for o in self.by_eng[e]:
                        toks = [ops[d]["tok"] for d in o["deps"] if ops[d]["tok"] is not None]
                        toks.sort(key=lambda t: -(t[-1] if t[0] != "x" else 1))
                        for t in toks:
                            wait_tok(t)
                        if o["cc"]:
                            ins = o["fn"](eng_h)
                            ins.then_inc(xsem[o["tok"][1]])
                        elif o["ndma"]:
                            s = dsem[e][o["dslot"]]
                            if o["dprev"]:
                                wait_tok(("d", e, o["dslot"], o["dprev"]))
                            ins = o["fn"](eng_h)
                            if not isinstance(ins, (list, tuple)):
                                ins = [ins]
                            assert len(ins) == o["ndma"], (len(ins), o["ndma"])
                            for i in ins:
                                i.then_inc(s, 16)
                        else:
                            ins = o["fn"](eng_h)
                            if o["signal"]:
                                ins.then_inc(csem[e], 1)
                    if e == "sp":
                        for d in final_wait_ops:
                            wait_tok(ops[d]["tok"])
                return body

            for e in ENGS:
                if self.by_eng[e] or e == "sp":
                    deco[e](make(e))
        return cnt, dcnt


class Arena:
    def __init__(self, S, ap, nwords):
        self.S = S
        self.ap = ap
        self.n = nwords
        self.top = 0
        self.peak = 0

    def mark(self):
        return self.top

    def release(self, m):
        self.top = m

    def tile(self, name, free_shape, dtype):
        nel = int(np.prod(free_shape))
        esz = 2 if dtype == BF16 else 4
        words = (nel * esz + 3) // 4
        words = (words + 7) // 8 * 8
        off = self.top
        self.top += words
        self.peak = max(self.peak, self.top)
        assert self.top <= self.n, ("SBUF arena overflow", name, self.top, self.n)
        v = self.ap[:, off:off + words]
        if dtype != F32:
            v = v.bitcast(dtype)
        v = v[:, 0:nel]
        if len(free_shape) == 2:
            v = v.rearrange("p (a b) -> p a b", a=free_shape[0])
        elif len(free_shape) == 3:
            v = v.rearrange("p (a b c) -> p a b c", a=free_shape[0], b=free_shape[1])
        elif len(free_shape) == 4:
            v = v.rearrange("p (a b c d) -> p a b c d", a=free_shape[0], b=free_shape[1], c=free_shape[2])
        return v, self.S.buf(name)


NSA_Q0 = 0
NSA_KV0 = 1024
NSA_GL0 = 1024 + 1536
GDN_QKV0 = NSA_GL0 + 48
GDN_A0 = GDN_QKV0 + 3072
GDN_B0 = GDN_A0 + 8
GDN_Z0 = GDN_B0 + 8
MERGE0 = GDN_Z0 + 1024
NFM = 11
NTMC = 400
HALO = 128


def _pT(v, k=None):
    v = np.asarray(v, np.float32).reshape(-1, 128)
    return np.ascontiguousarray(v.T)


def host_constants(T):
    NKT = T // 128
    c = {}
    c["ident"] = np.eye(128, dtype=np.float32)
    p = np.arange(128)[:, None]
    q = np.arange(128)[None, :]
    caus = np.where(p <= q, 0.0, NEG).astype(np.float32)
    wlow = np.where(p > q, 0.0, NEG).astype(np.float32)
    c["causrep"] = np.tile(caus, (1, 4))
    c["wlowrep"] = np.tile(wlow, (1, 4))
    cm = np.zeros((128, 17, 512), np.float32)
    for m in range(17):
        ok = (16 * p + 31) <= (128 * m + q)
        cm[:, m, :] = np.tile(np.where(ok, 0.0, NEG), (1, 4))
    c["cmpmask"] = cm.reshape(128, 17 * 512)
    e = np.zeros((128, NKT, 128), np.float32)
    for kt in range(NKT):
        for pp in range(128):
            j = 2 * kt + (1 if pp >= 64 else 0)
            if j < 128:
                e[j, kt, pp] = 1.0
    c["esel"] = e.reshape(128, NKT * 128)
    n = np.arange(512)
    cs = n * 16
    j = np.arange(128)
    ss = j * 64
    ov = np.clip(np.minimum(cs[:, None] + 32, ss[None, :] + 64) - np.maximum(cs[:, None], ss[None, :]), 0, None) / 32.0
    ova = np.concatenate([ov, np.ones((512, 1))], 1).astype(np.float32)
    c["ovaug"] = np.ascontiguousarray(ova.reshape(4, 128, 129).transpose(1, 0, 2)).reshape(128, 4 * 129)
    jj = np.arange(254)[None, :] - 126
    hi = (np.arange(128)[:, None] >= 64).astype(np.int64)
    ma = (jj <= hi - 2).astype(np.float32)
    mb = np.where(jj <= hi - 2, 0.0, np.where(jj <= hi, 1e6, -1.0)).astype(np.float32)
    c["selma"] = ma
    c["selmb"] = mb
    m_ = np.arange(128)[:, None]
    i_ = np.arange(128)[None, :]
    same = (m_ // 64) == (i_ // 64)
    c["tribd"] = (same & (m_ <= i_)).astype(np.float32)
    c["strictbd"] = (same & (m_ > i_)).astype(np.float32)
    c["blockones"] = same.astype(np.float32)
    c["half0"] = np.repeat((np.arange(128) < 64).astype(np.float32)[:, None], 128, 1)
    c["half1"] = np.repeat((np.arange(128) >= 64).astype(np.float32)[:, None], 128, 1)
    c["negL"] = np.where(same & (m_ > i_), 0.0, NEG).astype(np.float32)
    c["negU"] = np.where(same & (m_ <= i_), 0.0, NEG).astype(np.float32)
    c["ones"] = np.ones((128, 128), np.float32)
    return c


def host_core_inputs(inp, b, g, T):
    TB = T // 4
    f = lambda a: np.ascontiguousarray(np.asarray(a, np.float32))
    w_in = inp["w_in"][0]
    m = {}
    m["xb"] = f(inp["x"][b, :T])
    xo = np.zeros((HALO + TB, D), np.float32)
    xo[HALO:] = inp["x"][b, g * TB:(g + 1) * TB]
    if g > 0:
        xo[:HALO] = inp["x"][b, g * TB - HALO:g * TB]
    m["xown"] = xo
    m["cT"] = _pT(inp["c"][b])
    m["w_ada"] = f(inp["w_ada"][0])
    m["b_adaT"] = _pT(inp["b_ada"][0])
    m["nmixT"] = _pT(inp["norm_mix"][0])
    m["nffnT"] = _pT(inp["norm_ffn"][0])
    cols = []
    cols += list(range(NSA_Q0 + g * 256, NSA_Q0 + (g + 1) * 256))
    kv = lambda i: list(range(NSA_KV0 + (i * 4 + g) * 64, NSA_KV0 + (i * 4 + g) * 64 + 64))
    cols += kv(2) + kv(2)
    cols += kv(4) + kv(4)
    cols += kv(0) + kv(1)
    for part in range(3):
        for hh in range(2):
            base = GDN_QKV0 + part * 1024 + (2 * g + hh) * 128
            cols += list(range(base, base + 128))
    assert len(cols) == NFM * 128
    m["wfm"] = f(w_in[:, cols])
    cols = kv(3) + kv(5)
    cols += list(range(NSA_GL0 + g * 12, NSA_GL0 + (g + 1) * 12))
    cols += [GDN_A0 + 2 * g, GDN_A0 + 2 * g + 1, GDN_B0 + 2 * g, GDN_B0 + 2 * g + 1]
    cols += list(range(GDN_Z0 + g * 256, GDN_Z0 + (g + 1) * 256))
    assert len(cols) == NTMC
    m["wtm"] = f(w_in[:, cols])
    m["w1k"] = f(inp["nsa_ck_w1"][0])
    m["w1v"] = f(inp["nsa_cv_w1"][0])
    m["posT"] = f(np.concatenate([inp["nsa_pos_k"][0].T, inp["nsa_pos_v"][0].T], 0))
    m["b1T"] = f(np.concatenate([inp["nsa_ck_b1"][0].reshape(2, 128).T, inp["nsa_cv_b1"][0].reshape(2, 128).T], 1))
    w2k = inp["nsa_ck_w2"][0]
    m["w2kdup"] = f(np.concatenate([w2k, w2k], 1))
    m["w2v"] = f(inp["nsa_cv_w2"][0])
    b2k = inp["nsa_ck_b2"][0]
    m["b2kdupT"] = f(np.concatenate([b2k, b2k])[:, None])
    m["b2vbc"] = f(np.repeat(inp["nsa_cv_b2"][0][None, :], 128, 0))
    cw = inp["gdn_conv"][0]
    ct = np.zeros((128, 6, 4), np.float32)
    for part in range(3):
        for hh in range(2):
            base = part * 1024 + (2 * g + hh) * 128
            ct[:, part * 2 + hh, :] = cw[:, base:base + 128].T
    m["convT"] = ct.reshape(128, 24)
    m["alog"] = f(np.repeat(inp["gdn_a_log"][0][None, 2 * g:2 * g + 2], 128, 0))
    m["dtb"] = f(np.repeat(inp["gdn_dt_bias"][0][None, 2 * g:2 * g + 2], 128, 0))
    m["gnormbc"] = f(np.repeat(inp["gdn_norm"][0][None, :], 128, 0))
    def blk(w, nk):
        nout = w.shape[1]
        a = np.asarray(w, np.float32).reshape(nk, 128, nout // 128, 128)
        return np.ascontiguousarray(a.transpose(2, 1, 0, 3)).reshape(nout // 128, 128, nk * 128)
    m["wmerge_b"] = blk(w_in[:, MERGE0:MERGE0 + 2048], 8)
    m["wpn_b"] = blk(inp["w_proj_nsa"][0], 8)
    m["wpg_b"] = blk(inp["w_proj_gdn"][0], 8)
    m["wout_b"] = blk(inp["w_out"][0], 8)
    m["wup_b"] = blk(inp["ffn_up"][0], 8)
    m["wdn_b"] = blk(inp["ffn_down"][0], 22)
    fc = inp["ffn_conv"][0]
    m["fconvT"] = f(fc.reshape(3, 44, 128).transpose(2, 1, 0)).reshape(128, 132)
    m["fconvbT"] = _pT(inp["ffn_conv_b"][0])
    m["nfbc"] = f(np.repeat(inp["norm_final"][None, :], 128, 0))
    m["tokoff"] = np.array([[g * TB, max(g * TB - HALO, 0)]], np.int32)
    m["halomask"] = np.full((128, 1), 0.0 if g == 0 else 1.0, np.float32)
    return m


ARENA_WORDS = 49152


def build(T, dbg=(), phases=("A0", "A1", "A2", "A3", "G", "X", "B"), ystatic=False):
    nc = bass.Bass("TRN2", target_bir_lowering=False)
    S = Sched(nc)
    TB = T // 4
    NT = T // 512
    NTT = T // 128
    st = contextlib.ExitStack()

    def din(name, shape, dt=F32):
        return nc.dram_tensor(name, list(shape), dt, kind="ExternalInput").ap()

    def dout(name, shape, dt=F32):
        return nc.dram_tensor(name, list(shape), dt, kind="ExternalOutput").ap()

    def dscr(name, shape, dt):
        return nc.dram_tensor(name, list(shape), dt)

    cshapes = {k: v.shape for k, v in host_constants(T).items()}
    cst_d = {k: din("c_" + k, s) for k, s in cshapes.items()}
    xb = din("xb", [T, D])
    xown = din("xown", [HALO + TB, D])
    cT_d = din("cT", [128, 8])
    w_ada = din("w_ada", [D, 6 * D])
    b_adaT_d = din("b_adaT", [128, 48])
    nmixT_d = din("nmixT", [128, 8])
    nffnT_d = din("nffnT", [128, 8])
    wfm_d = din("wfm", [D, NFM * 128])
    wtm_d = din("wtm", [D, NTMC])

    w1k_d = din("w1k", [2048, 256])
    w1v_d = din("w1v", [2048, 256])
    posT_d = din("posT", [128, 32])
    b1T_d = din("b1T", [128, 4])
    w2kdup_d = din("w2kdup", [256, 128])
    w2v_d = din("w2v", [256, 64])
    b2kdupT_d = din("b2kdupT", [128, 1])
    b2vbc_d = din("b2vbc", [128, 64])
    ya_scr = dscr("ya_scr", [256, T], BF16)
    yb_scr = dscr("yb_scr", [256, T], BF16)
    convT_d = din("convT", [128, 24])
    alog_d = din("alog", [128, 2])
    dtb_d = din("dtb", [128, 2])
    gnormbc_d = din("gnormbc", [128, 128])
    wmerge_b = din("wmerge_b", [16, 128, 1024])
    wpn_b = din("wpn_b", [8, 128, 1024])
    wpg_b = din("wpg_b", [8, 128, 1024])
    wout_b = din("wout_b", [8, 128, 1024])
    wup_b = din("wup_b", [44, 128, 1024])
    wdn_b = din("wdn_b", [8, 128, 22 * 128])
    fconvT_d = din("fconvT", [128, 132])
    fconvbT_d = din("fconvbT", [128, 44])
    nfbc_d = din("nfbc", [128, D])
    tokoff_d = din("tokoff", [1, 2], I32)
    halomask_d = din("halomask", [128, 1])
    ya_all = dscr("ya_all", [1024, T], BF16)
    yb_all = dscr("yb_all", [1024, T], BF16)
    out_d = dout("out", [TB, D]) if "B" in phases else None
    if ystatic:
        ya_in = din("ya_in", [1024, HALO + TB], BF16)
        yb_in = din("yb_in", [1024, HALO + TB], BF16)
    gdn_raw = dscr("gdn_raw", [6, 128, T], BF16)
    z_scr = dscr("z_scr", [T, 256], BF16)

    outs = {}
    arena_t = st.enter_context(nc.sbuf_tensor("arena", [128, ARENA_WORDS], F32))
    ar = Arena(S, arena_t[:, :], ARENA_WORDS)
    psum = []
    for i in range(8):
        t = st.enter_context(nc.psum_tensor("ps%d" % i, [128, 512], F32))
        psum.append((t[:, :], S.buf("ps%d" % i, True)))

    def dma(q, out, in_, reads=(), writes=()):
        return S.op(q, lambda e: e.dma_start(out=out, in_=in_), reads=reads, writes=writes, ndma=1)

    def mm(out, lhsT, rhs, start, stop, reads, writes):
        return S.op("pe", lambda e: e.matmul(out, lhsT=lhsT, rhs=rhs, start=start, stop=stop,
                                             skip_group_check=True), reads=reads, writes=writes)

    def tr(out, in_, ident, reads, writes):
        return S.op("pe", lambda e: e.transpose(out, in_, ident), reads=reads, writes=writes)

    def act(out, in_, func, reads, writes, bias=None, scale=None, accum_out=None, eng="act"):
        kw = {}
        if bias is not None:
            kw["bias"] = bias
        if scale is not None:
            kw["scale"] = scale
        if accum_out is not None:
            kw["accum_out"] = accum_out
        return S.op("act", lambda e: e.activation(out=out, in_=in_, func=func, **kw), reads=reads, writes=writes)

    def ts(eng, out, in0, s1, s2, op0, op1, reads, writes):
        if op1 is None:
            return S.op(eng, lambda e: e.tensor_scalar(out=out, in0=in0, scalar1=s1, scalar2=None, op0=op0),
                        reads=reads, writes=writes)
        return S.op(eng, lambda e: e.tensor_scalar(out=out, in0=in0, scalar1=s1, scalar2=s2, op0=op0, op1=op1),
                    reads=reads, writes=writes)

    def tt(eng, out, in0, in1, op, reads, writes):
        return S.op(eng, lambda e: e.tensor_tensor(out=out, in0=in0, in1=in1, op=op), reads=reads, writes=writes)

    def stt(out, in0, scalar, in1, op0, op1, reads, writes):
        return S.op("dve", lambda e: e.scalar_tensor_tensor(out=out, in0=in0, scalar=scalar, in1=in1, op0=op0, op1=op1),
                    reads=reads, writes=writes)

    def cp(eng, out, in_, reads, writes):
        if eng == "act":
            return S.op("act", lambda e: e.copy(out=out, in_=in_), reads=reads, writes=writes)
        return S.op(eng, lambda e: e.tensor_copy(out=out, in_=in_), reads=reads, writes=writes)

    def memset(eng, ap, val, writes):
        return S.op(eng, lambda e: e.memset(ap, val), writes=writes)

    final_ops = []

    def dump(name, src_ap, src_buf, shape, dt=F32):
        o = dout("dbg_" + name, shape, dt)
        final_ops.append(dma("sp", o, src_ap, reads=[src_buf], writes=[S.buf("dbg_" + name)]))

    ident_b, b_ident_b = ar.tile("ident_b", [128], BF16)
    ident_f, b_ident_f = ar.tile("ident_f", [128], F32)
    modT, b_modT = ar.tile("modT", [48], F32)
    amix, b_amix = ar.tile("amix", [8], F32)
    affn, b_affn = ar.tile("affn", [8], F32)
    nmixT, b_nmixT = ar.tile("nmixT", [8], F32)
    nffnT, b_nffnT = ar.tile("nffnT", [8], F32)
    cTt, b_cT = ar.tile("cT", [8], F32)
    badaT, b_badaT = ar.tile("badaT", [48], F32)
    epsc, b_epsc = ar.tile("epsc", [1], F32)
    dma("pool", ident_b, cst_d["ident"], writes=[b_ident_b])
    dma("sp", ident_f, cst_d["ident"], writes=[b_ident_f])
    dma("sp", cTt, cT_d, writes=[b_cT])
    dma("sp", badaT, b_adaT_d, writes=[b_badaT])
    dma("sp", nmixT, nmixT_d, writes=[b_nmixT])
    dma("sp", nffnT, nffnT_d, writes=[b_nffnT])
    memset("dve", epsc, EPS, [b_epsc])

    if "A0" in phases:
        mk = ar.mark()
        wa = [ar.tile("wa%d" % i, [8, 768], F32) for i in range(2)]
        ps, pb = psum[0]
        w_ada_v = w_ada.rearrange("(kc p) n -> p kc n", p=128)
        for blk in range(8):
            wt, wb = wa[blk % 2]
            dma("sp", wt, w_ada_v[:, :, blk * 768:(blk + 1) * 768], writes=[wb])
            for jj in range(6):
                j = blk * 6 + jj
                for k in range(8):
                    mm(ps[:, j:j + 1], wt[:, k, jj * 128:(jj + 1) * 128], cTt[:, k:k + 1], k == 0, k == 7,
                       [wb, b_cT], [pb])
        tt("dve", modT, ps[:, 0:48], badaT, ALU.add, [pb, b_badaT], [b_modT])
        stt(amix, modT[:, 8:16], 1.0, nmixT, ALU.add, ALU.mult, [b_modT, b_nmixT], [b_amix])
        stt(affn, modT[:, 32:40], 1.0, nffnT, ALU.add, ALU.mult, [b_modT, b_nffnT], [b_affn])
        ar.release(mk)
        S.barrier()
        if "modT" in dbg:
            dump("modT", modT, b_modT, [128, 48])

    ab_tm = ar.tile("ab_tm", [NTT, 4], F32)
    mk_nsa = ar.mark()
    QT = ar.tile("QT", [2, T], BF16)
    KsT = ar.tile("KsT", [T], BF16)
    KwT = ar.tile("KwT", [T], BF16)
    KcVcT = ar.tile("KcVcT", [T], BF16)
    Vsw = ar.tile("Vsw", [NTT, 2, 65], BF16)
    gates_tm = ar.tile("gates_tm", [NTT, 12], F32)
    b_QT = [S.buf("QT%d" % i) for i in range(NT)]
    b_KsT = [S.buf("KsT%d" % i) for i in range(NT)]
    b_KwT = [S.buf("KwT%d" % i) for i in range(NT)]
    b_KcVcT = [S.buf("KcVcT%d" % i) for i in range(NT)]
    b_Vsw = [S.buf("Vsw%d" % i) for i in range(NT)]
    b_gates = [S.buf("gates%d" % i) for i in range(NT)]
    b_ab = [S.buf("ab%d" % i) for i in range(NT)]
    b_graw = [S.buf("graw%d" % i) for i in range(NT)]
    b_zscr = [S.buf("zscr%d" % i) for i in range(NT)]
    memset("pool", Vsw[0][:, :, :, 64:65], 1.0, b_Vsw)

    if "A1" in phases:
        mk = ar.mark()
        wfm, b_wfm = ar.tile("wfm", [8, NFM * 128], BF16)
        wtm, b_wtm = ar.tile("wtm", [8, NTMC], BF16)
        for k in range(8):
            dma("pool", wfm[:, k, :], wfm_d[k * 128:(k + 1) * 128, :], writes=[b_wfm])
            dma("pool", wtm[:, k, :], wtm_d[k * 128:(k + 1) * 128, :], writes=[b_wtm])
        xt = [ar.tile("xt%d" % i, [D], F32) for i in range(3)]
        xn = [ar.tile("xn%d" % i, [D], BF16) for i in range(2)]
        sq = ar.tile("sqjunk", [D], BF16)
        ssq = [ar.tile("ssq%d" % i, [2], F32) for i in range(2)]
        hT = [ar.tile("hT%d" % i, [8, 512], BF16) for i in range(2)]
        gst = [ar.tile("gst%d" % i, [512], BF16) for i in range(4)]
        zst = [ar.tile("zst%d" % i, [256], BF16) for i in range(2)]
        nsub = 0
        nev = 0
        for i in range(NT):
            h_t, h_b = hT[i % 2]
            for s_ in range(4):
                tt_i = 4 * i + s_
                x_t, x_b = xt[nsub % 3]
                xn_t, xn_b = xn[nsub % 2]
                ss_t, ss_b = ssq[nsub % 2]
                dma("sp", x_t, xb[tt_i * 128:(tt_i + 1) * 128, :], writes=[x_b])
                act(sq[0], x_t, AF.Square, [x_b], [sq[1], ss_b], accum_out=ss_t[:, 0:1])
                act(ss_t[:, 1:2], ss_t[:, 0:1], AF.Ln, [ss_b, b_epsc], [ss_b], bias=epsc[:, 0:1], scale=1.0 / D)
                act(ss_t[:, 1:2], ss_t[:, 1:2], AF.Exp, [ss_b], [ss_b], scale=-0.5)
                act(xn_t, x_t, AF.Copy, [x_b, ss_b], [xn_b], scale=ss_t[:, 1:2])
                pt, ptb = psum[nsub % 2]
                ptv = pt.bitcast(BF16)
                for k in range(8):
                    tr(ptv[:, k * 128:(k + 1) * 128], xn_t[:, k * 128:(k + 1) * 128], ident_b, [xn_b, b_ident_b], [ptb])
                for k in range(8):
                    o_ = h_t[:, k, s_ * 128:(s_ + 1) * 128]
                    i_ = ptv[:, k * 128:(k + 1) * 128]
                    if k % 2 == 0:
                        act(o_, i_, AF.Identity, [ptb, b_amix, b_modT], [h_b], bias=modT[:, k:k + 1], scale=amix[:, k:k + 1])
                    else:
                        ts("dve", o_, i_, amix[:, k:k + 1], modT[:, k:k + 1], ALU.mult, ALU.add, [ptb, b_amix, b_modT], [h_b])
                nsub += 1
            cols = slice(i * 512, (i + 1) * 512)
            for gi in range(NFM):
                ps, pb = psum[2 + gi % 3]
                for k in range(8):
                    mm(ps[:, :], wfm[:, k, gi * 128:(gi + 1) * 128], h_t[:, k, :], k == 0, k == 7, [b_wfm, h_b], [pb])
                eng = "act" if nev % 2 == 0 else "dve"
                nev += 1
                if gi < 2:
                    dst, dbf = QT[0][:, gi, cols], b_QT[i]
                    if eng == "act":
                        act(dst, ps[:, :], AF.Copy, [pb], [dbf], scale=0.125)
                    else:
                        ts("dve", dst, ps[:, :], 0.125, None, ALU.mult, None, [pb], [dbf])
                    continue
                if gi == 2:
                    dst, dbf = KsT[0][:, cols], b_KsT[i]
                elif gi == 3:
                    dst, dbf = KwT[0][:, cols], b_KwT[i]
                elif gi == 4:
                    dst, dbf = KcVcT[0][:, cols], b_KcVcT[i]
                else:
                    dst, dbf = gst[(gi - 5) % 4]
                cp(eng, dst, ps[:, :], [pb], [dbf])
                if gi >= 5:
                    dma("sp", gdn_raw[gi - 5, :, cols], dst, reads=[dbf], writes=[b_graw[i]])
            for s_ in range(4):
                tt_i = 4 * i + s_
                ps, pb = psum[5 + s_ % 2]
                for k in range(8):
                    mm(ps[:, 0:NTMC], h_t[:, k, s_ * 128:(s_ + 1) * 128], wtm[:, k, :], k == 0, k == 7, [b_wtm, h_b], [pb])
                cp("dve", Vsw[0][:, tt_i, :, 0:64], ps[:, 0:128].rearrange("p (a d) -> p a d", a=2), [pb], [b_Vsw[i]])
                gt_ = gates_tm[0][:, tt_i, :]
                act(gt_, ps[:, 128:140], AF.Exp, [pb], [b_gates[i]], scale=-1.0)
                ts("dve", gt_, gt_, 1.0, None, ALU.add, None, [b_gates[i]], [b_gates[i]])
                S.op("dve", lambda e, gt_=gt_: e.reciprocal(out=gt_, in_=gt_), reads=[b_gates[i]], writes=[b_gates[i]])
                cp("dve", ab_tm[0][:, tt_i, :], ps[:, 140:144], [pb], [b_ab[i]])
                z_t, z_b = zst[s_ % 2]
                cp("act", z_t, ps[:, 144:400], [pb], [z_b])
                dma("sp", z_scr[tt_i * 128:(tt_i + 1) * 128, :], z_t, reads=[z_b], writes=[b_zscr[i]])
            if "hT" in dbg and i == 1:
                dump("hT", h_t, h_b, [128, 8, 512], BF16)
        ar.release(mk)
        S.barrier()
        if "A1" in dbg:
            dump("QT", QT[0], b_QT[0], [128, 2, T], BF16)
            dump("KsT", KsT[0], b_KsT[0], [128, T], BF16)
            dump("KcVcT", KcVcT[0], b_KcVcT[0], [128, T], BF16)
            dump("Vsw", Vsw[0], b_Vsw[0], [128, NTT, 2, 65], BF16)
            dump("gates", gates_tm[0], b_gates[0], [128, NTT, 12])
            dump("ab", ab_tm[0], b_ab[0], [128, NTT, 4])
            g_o = dout("dbg_graw", [6, 128, T], BF16)
            final_ops.append(dma("sp", g_o, gdn_raw[:, :, :], reads=b_graw, writes=[S.buf("x")]))
            z_o = dout("dbg_z", [T, 256], BF16)
            final_ops.append(dma("sp", z_o, z_scr[:, :], reads=b_zscr, writes=[S.buf("x2")]))


    NC = T // 16 - 1
    NCT = (NC + 127) // 128
    NCP = NCT * 128
    KCT = ar.tile("KCT", [NCP], BF16)
    VCaug = ar.tile("VCaug", [NCT, 65], BF16)
    if "A2" in phases:
        mk = ar.mark()
        w1kv, b_w1kv = ar.tile("w1kv", [32, 256], BF16)
        dma("pool", w1kv[0:64], w1k_d.rearrange("(l d) h -> d l h", d=64), writes=[b_w1kv])
        dma("pool", w1kv[64:128], w1v_d.rearrange("(l d) h -> d l h", d=64), writes=[b_w1kv])
        posT, b_posT = ar.tile("posT", [32], BF16)
        dma("pool", posT, posT_d, writes=[b_posT])
        b1T, b_b1T = ar.tile("b1T", [4], F32)
        dma("sp", b1T, b1T_d, writes=[b_b1T])
        w2kd, b_w2kd = ar.tile("w2kd", [2, 128], BF16)
        dma("pool", w2kd, w2kdup_d.rearrange("(hc p) c -> p hc c", p=128), writes=[b_w2kd])
        w2v, b_w2v = ar.tile("w2v", [2, 64], BF16)
        dma("pool", w2v, w2v_d.rearrange("(hc p) c -> p hc c", p=128), writes=[b_w2v])
        b2kd, b_b2kd = ar.tile("b2kd", [1], F32)
        dma("sp", b2kd, b2kdupT_d, writes=[b_b2kd])
        b2vbc, b_b2vbc = ar.tile("b2vbc", [64], F32)
        dma("sp", b2vbc, b2vbc_d, writes=[b_b2vbc])
        cb1, b_cb1 = ar.tile("cb1", [4], F32)
        ps, pb = psum[0]
        for kv in range(2):
            pr = slice(64 * kv, 64 * kv + 64)
            for hc in range(2):
                idx = kv * 2 + hc
                for l in range(32):
                    mm(ps[:, idx:idx + 1], w1kv[pr, l, hc * 128:(hc + 1) * 128], posT[pr, l:l + 1], l == 0, l == 31,
                       [b_w1kv, b_posT], [pb])
        tt("dve", cb1, ps[:, 0:4], b1T, ALU.add, [pb, b_b1T], [b_cb1])
        gel = [[ar.tile("gel%d%d" % (kv, hc), [NCP], BF16) for hc in range(2)] for kv in range(2)]
        xbt = ar.tile("xbt", [NCP], F32)
        x2t = ar.tile("x2t", [NCP], F32)
        ngr = 0
        for kv in range(2):
            pr = slice(64 * kv, 64 * kv + 64)
            for hc in range(2):
                idx = kv * 2 + hc
                ps, pb = psum[1 + ngr % 2]
                ngr += 1
                for l in range(32):
                    mm(ps[:, 0:NC], w1kv[pr, l, hc * 128:(hc + 1) * 128], KcVcT[0][pr, l:l + 16 * (NC - 1) + 1:16], l == 0, l == 31,
                       [b_w1kv] + b_KcVcT, [pb])
                g_t, g_b = gel[kv][hc]
                memset("pool", g_t, 0.0, [g_b])
                xb_, x2_ = xbt[0][:, 0:NC], x2t[0][:, 0:NC]
                act(xb_, ps[:, 0:NC], AF.Identity, [pb, b_cb1], [xbt[1]], bias=cb1[:, idx:idx + 1])
                tt("dve", x2_, xb_, xb_, ALU.mult, [xbt[1]], [x2t[1]])
                ts("dve", x2_, x2_, 0.044715, 1.0, ALU.mult, ALU.add, [x2t[1]], [x2t[1]])
                tt("dve", x2_, x2_, xb_, ALU.mult, [x2t[1], xbt[1]], [x2t[1]])
                act(x2_, x2_, AF.Exp, [x2t[1]], [x2t[1]], scale=-1.5957691216)
                ts("dve", x2_, x2_, 1.0, None, ALU.add, None, [x2t[1]], [x2t[1]])
                S.op("dve", lambda e, x2_=x2_: e.reciprocal(out=x2_, in_=x2_), reads=[x2t[1]], writes=[x2t[1]])
                tt("dve", g_t[:, 0:NC], xb_, x2_, ALU.mult, [xbt[1], x2t[1]], [g_b])
        memset("pool", KCT[0], 0.0, [KCT[1]])
        ps, pb = psum[3]
        for hc in range(2):
            mm(ps[:, 0:NC], w2kd[:, hc, :], gel[0][hc][0][:, 0:NC], hc == 0, hc == 1, [b_w2kd, gel[0][hc][1]], [pb])
        act(KCT[0][:, 0:NC], ps[:, 0:NC], AF.Identity, [pb, b_b2kd], [KCT[1]], bias=b2kd[:, 0:1])
        memset("pool", VCaug[0][:, :, 64:65], 1.0, [VCaug[1]])
        for nt in range(NCT):
            ps, pb = psum[4 + nt % 2]
            for hc in range(2):
                mm(ps[:, 0:64], gel[1][hc][0][:, nt * 128:(nt + 1) * 128], w2v[:, hc, :], hc == 0, hc == 1,
                   [b_w2v, gel[1][hc][1]], [pb])
            tt("dve", VCaug[0][:, nt, 0:64], ps[:, 0:64], b2vbc, ALU.add, [pb, b_b2vbc], [VCaug[1]])
        ar.release(mk)
        S.barrier()
        if "A2" in dbg:
            dump("KCT", KCT[0], KCT[1], [128, NCP], BF16)
            dump("VC", VCaug[0], VCaug[1], [128, NCT, 65], BF16)

    if "A3" in phases:
        mk = ar.mark()
        NKT = NTT
        causrep, b_caus = ar.tile("causrep", [512], BF16)
        wlowrep, b_wlow = ar.tile("wlowrep", [512], BF16)
        cmpmask, b_cmpm = ar.tile("cmpmask", [17, 512], BF16)
        esel, b_esel = ar.tile("esel", [NKT, 128], BF16)
        ovaug, b_ov = ar.tile("ovaug", [4, 129], BF16)
        selma, b_selma = ar.tile("selma", [254], F32)
        selmb, b_selmb = ar.tile("selmb", [254], F32)
        dma("pool", causrep, cst_d["causrep"], writes=[b_caus])
        dma("pool", wlowrep, cst_d["wlowrep"], writes=[b_wlow])
        dma("pool", cmpmask, cst_d["cmpmask"].rearrange("p (m c) -> p m c", m=17), writes=[b_cmpm])
        dma("pool", esel, cst_d["esel"].rearrange("p (k c) -> p k c", k=NKT), writes=[b_esel])
        dma("pool", ovaug, cst_d["ovaug"].rearrange("p (k c) -> p k c", k=4), writes=[b_ov])
        dma("sp", selma, cst_d["selma"], writes=[b_selma])
        dma("sp", selmb, cst_d["selmb"], writes=[b_selmb])
        b_ya = S.buf("ya_scr")
        PT = [ar.tile("PT%d" % i, [512], BF16) for i in range(3)]
        Oev = [ar.tile("Oev%d" % i, [512], F32) for i in range(3)]
        impacc = ar.tile("impacc", [128], F32)
        score = ar.tile("score", [128], F32)
        score2 = ar.tile("score2", [128], F32)
        top8 = ar.tile("top8", [16], F32)
        zc = ar.tile("zc", [16], F32)
        fz = ar.tile("fz", [12], F32)
        selt = ar.tile("selt", [128], BF16)
        negselT = ar.tile("negselT", [512], BF16)
        ytmp = ar.tile("ytmp", [256], F32)
        ytm = ar.tile("ytm", [256], BF16)
        yst = [ar.tile("yst%d" % i, [2, 512], BF16) for i in range(2)]
        nsc = 0
        npt = 0
        psOc, psOs, psOw = psum[2], psum[3], psum[4]
        psI = [psum[5], psum[6]]
        psM = psum[7]
        hp = [slice(0, 64), slice(64, 128)]

        def heads(ps, pb, Ksrc, kbufs, kt, qt, first_start):
            qcols = slice(qt * 128, (qt + 1) * 128)
            for h in range(4):
                mm(ps[:, h * 128:(h + 1) * 128], Ksrc[hp[h % 2], kt * 128:(kt + 1) * 128], QT[0][hp[h % 2], h // 2, qcols],
                   first_start and h == 0, h == 3, kbufs + [b_QT[qt // 4]], [pb])

        for qt in range(NTT):
            nct = min(NCT, qt // 16 + 1)
            for kt in range(nct):
                m_ = qt - 16 * kt
                ps, pb = psum[nsc % 2]
                nsc += 1
                if m_ <= 16:
                    mm(ps[:, :], ident_b, cmpmask[:, m_, :], True, False, [b_ident_b, b_cmpm], [pb])
                    heads(ps, pb, KCT[0], [KCT[1]], kt, qt, False)
                else:
                    heads(ps, pb, KCT[0], [KCT[1]], kt, qt, True)
                p_t, p_b = PT[npt % 3]
                npt += 1
                act(p_t, ps[:, :], AF.Exp, [pb], [p_b])
                mm(psOc[0][0:65, :], VCaug[0][:, kt, :], p_t, kt == 0, kt == nct - 1, [VCaug[1], p_b], [psOc[1]])
                for h in range(4):
                    pI, pIb = psI[h // 2]
                    mm(pI[:, (h % 2) * 129:(h % 2) * 129 + 129], p_t[:, h * 128:(h + 1) * 128], ovaug[:, kt, :],
                       kt == 0 and h % 2 == 0, kt == nct - 1, [p_b, b_ov], [pIb])
            for bnk in range(2):
                pI, pIb = psI[bnk]
                zsrc = pI[:, 128:258:129]
                ts("dve", zc[0][:, 2 * bnk:2 * bnk + 2], zsrc, 1e-30, None, ALU.add, None, [pIb], [zc[1]])
            S.op("dve", lambda e: e.reciprocal(out=zc[0][:, 4:8], in_=zc[0][:, 0:4]), reads=[zc[1]], writes=[zc[1]])
            for h in range(4):
                pI, pIb = psI[h // 2]
                src = pI[:, (h % 2) * 129:(h % 2) * 129 + 128]
                if h == 0:
                    ts("dve", impacc[0], src, zc[0][:, 4:5], None, ALU.mult, None, [pIb, zc[1]], [impacc[1]])
                else:
                    stt(impacc[0], src, zc[0][:, 4 + h:5 + h], impacc[0], ALU.mult, ALU.add, [pIb, zc[1], impacc[1]], [impacc[1]])
            c0 = 126 - 2 * qt
            tt("dve", score[0], impacc[0], selma[:, c0:c0 + 128], ALU.mult, [impacc[1], b_selma], [score[1]])
            tt("dve", score[0], score[0], selmb[:, c0:c0 + 128], ALU.add, [score[1], b_selmb], [score[1]])
            memset("dve", score[0][:, 0:1], 1e6, [score[1]])
            S.op("dve", lambda e: e.max(out=top8[0][:, 0:8], in_=score[0]), reads=[score[1]], writes=[top8[1]])
            S.op("dve", lambda e: e.match_replace(out=score2[0], in_to_replace=top8[0][:, 0:8], in_values=score[0],
                                                  imm_value=-2.0), reads=[score[1], top8[1]], writes=[score2[1]])
            S.op("dve", lambda e: e.max(out=top8[0][:, 8:16], in_=score2[0]), reads=[score2[1]], writes=[top8[1]])
            ts("dve", top8[0][:, 15:16], top8[0][:, 15:16], 0.0, None, ALU.max, None, [top8[1]], [top8[1]])
            ts("dve", selt[0], score[0], top8[0][:, 15:16], None, ALU.is_ge, None, [score[1], top8[1]], [selt[1]])
            pMv = psM[0].bitcast(BF16)
            tr(pMv[:, 0:128], selt[0], ident_b, [selt[1], b_ident_b], [psM[1]])
            ts("dve", negselT[0].rearrange("p (r q) -> p r q", r=4), pMv[:, 0:128].unsqueeze(1).broadcast_to([128, 4, 128]),
               -1.0, -NEG, ALU.add, ALU.mult, [psM[1]], [negselT[1]])
            for kt in range(qt + 1):
                ps, pb = psum[nsc % 2]
                nsc += 1
                mm(ps[:, :], esel[:, kt, :], negselT[0], True, False, [b_esel, negselT[1]], [pb])
                if kt == qt:
                    mm(ps[:, :], ident_b, causrep, False, False, [b_ident_b, b_caus], [pb])
                heads(ps, pb, KsT[0], [b_KsT[kt // 4]], kt, qt, False)
                p_t, p_b = PT[npt % 3]
                npt += 1
                act(p_t, ps[:, :], AF.Exp, [pb], [p_b])
                mm(psOs[0][0:65, :], Vsw[0][:, kt, 0, :], p_t, kt == 0, kt == qt, [b_Vsw[kt // 4], p_b], [psOs[1]])
            kts = list(range(max(0, qt - 4), qt + 1))
            for kt in kts:
                ps, pb = psum[nsc % 2]
                nsc += 1
                first = True
                if kt == qt:
                    mm(ps[:, :], ident_b, causrep, True, False, [b_ident_b, b_caus], [pb])
                    first = False
                if kt == qt - 4:
                    mm(ps[:, :], ident_b, wlowrep, first, False, [b_ident_b, b_wlow], [pb])
                    first = False
                heads(ps, pb, KwT[0], [b_KwT[kt // 4]], kt, qt, first)
                p_t, p_b = PT[npt % 3]
                npt += 1
                act(p_t, ps[:, :], AF.Exp, [pb], [p_b])
                mm(psOw[0][0:65, :], Vsw[0][:, kt, 1, :], p_t, kt == kts[0], kt == qt, [b_Vsw[kt // 4], p_b], [psOw[1]])
            for bi, pO in enumerate((psOc, psOs, psOw)):
                if bi == 1:
                    cp("dve", Oev[bi][0][0:65, :], pO[0][0:65, :], [pO[1]], [Oev[bi][1]])
                else:
                    cp("act", Oev[bi][0][0:65, :], pO[0][0:65, :], [pO[1]], [Oev[bi][1]])
            for h in range(4):
                for br in range(3):
                    idx = h * 3 + br
                    pF, pFb = psI[0] if idx < 7 else psI[1]
                    off = (idx if idx < 7 else idx - 7) * 65
                    tr(pF[:, off:off + 65], Oev[br][0][0:65, h * 128:(h + 1) * 128], ident_f[0:65, 0:65],
                       [Oev[br][1], b_ident_f], [pFb])
            ts("dve", zc[0][:, 0:7], psI[0][0][:, 64:64 + 6 * 65 + 1:65], 1e-30, None, ALU.add, None, [psI[0][1]], [zc[1]])
            ts("dve", zc[0][:, 7:12], psI[1][0][:, 64:64 + 4 * 65 + 1:65], 1e-30, None, ALU.add, None, [psI[1][1]], [zc[1]])
            S.op("dve", lambda e: e.reciprocal(out=fz[0], in_=zc[0][:, 0:12]), reads=[zc[1]], writes=[fz[1]])
            tt("dve", fz[0], fz[0], gates_tm[0][:, qt, :], ALU.mult, [fz[1], b_gates[qt // 4]], [fz[1]])
            for h in range(4):
                for br in range(3):
                    idx = h * 3 + br
                    pF, pFb = psI[0] if idx < 7 else psI[1]
                    off = (idx if idx < 7 else idx - 7) * 65
                    dst = ytm[0][:, h * 64:(h + 1) * 64] if br == 2 else ytmp[0][:, h * 64:(h + 1) * 64]
                    dbf = ytm[1] if br == 2 else ytmp[1]
                    if br == 0:
                        ts("dve", dst, pF[:, off:off + 64], fz[0][:, idx:idx + 1], None, ALU.mult, None, [pFb, fz[1]], [dbf])
                    else:
                        stt(dst, pF[:, off:off + 64], fz[0][:, idx:idx + 1], ytmp[0][:, h * 64:(h + 1) * 64],
                            ALU.mult, ALU.add, [pFb, fz[1], ytmp[1]], [dbf])
            y_s, y_sb = yst[(qt // 4) % 2]
            for pr_ in range(2):
                tr(pMv[:, 128 + pr_ * 128:256 + pr_ * 128], ytm[0][:, pr_ * 128:(pr_ + 1) * 128], ident_b,
                   [ytm[1], b_ident_b], [psM[1]])
            cp("act", y_s[:, :, (qt % 4) * 128:(qt % 4 + 1) * 128], pMv[:, 128:384].rearrange("p (a q) -> p a q", a=2),
               [psM[1]], [y_sb])
            if qt % 4 == 3:
                c_ = (qt // 4) * 512
                dma("sp", ya_scr.rearrange("(a p) t -> p a t", p=128)[:, :, c_:c_ + 512], y_s, reads=[y_sb], writes=[b_ya])
        ar.release(mk)
        S.barrier()
        if "A3" in dbg:
            o = dout("dbg_ya", [256, T], BF16)
            final_ops.append(dma("sp", o, ya_scr[:, :], reads=[b_ya], writes=[S.buf("x3")]))

    ar.release(mk_nsa)
    S.barrier()
    if "G" in phases:
        mk = ar.mark()
        N2 = NTT * 2
        cn = {}
        for nm in ("tribd", "strictbd", "blockones", "half0", "half1", "negL", "negU", "ones"):
            cn[nm] = ar.tile("c_" + nm, [128], F32)
            dma("sp", cn[nm][0], cst_d[nm], writes=[cn[nm][1]])
        ones_b = ar.tile("ones_b", [128], BF16)
        dma("pool", ones_b[0], cst_d["ones"], writes=[ones_b[1]])
        convT = ar.tile("convT", [6, 4], F32)
        dma("sp", convT[0], convT_d.rearrange("p (a b) -> p a b", a=6), writes=[convT[1]])
        alog = ar.tile("alog", [2], F32)
        dtb = ar.tile("dtb", [2], F32)
        gnormbc = ar.tile("gnormbc", [128], F32)
        dma("sp", alog[0], alog_d, writes=[alog[1]])
        dma("sp", dtb[0], dtb_d, writes=[dtb[1]])
        dma("sp", gnormbc[0], gnormbc_d, writes=[gnormbc[1]])
        onec = ar.tile("onec", [1], F32)
        memset("dve", onec[0], 1.0, [onec[1]])
        qkvn = [ar.tile("qkvn%d" % gi, [T], BF16) for gi in range(6)]
        b_qkvn = [[S.buf("qkvn%d_%d" % (gi, i)) for i in range(NT)] for gi in range(6)]
        rawt = [ar.tile("rawt%d" % i, [515], BF16) for i in range(3)]
        cacc = [ar.tile("cacc%d" % i, [512], F32) for i in range(2)]
        sact = [ar.tile("sact%d" % i, [512], F32) for i in range(2)]
        sqb = [ar.tile("sqb%d" % i, [512], BF16) for i in range(2)]
        rst = [ar.tile("rst%d" % i, [512], F32) for i in range(2)]
        nld = 0
        for i in range(NT):
            cols = slice(i * 512, (i + 1) * 512)
            for gi in range(6):
                r_t, r_b = rawt[nld % 3]
                c_t, c_b = cacc[nld % 2]
                s_t, s_b = sact[nld % 2]
                nld += 1
                if i == 0:
                    memset("pool", r_t[:, 0:3], 0.0, [r_b])
                    dma("sp", r_t[:, 3:515], gdn_raw[gi, :, 0:512], reads=[b_graw[0]], writes=[r_b])
                else:
                    dma("sp", r_t, gdn_raw[gi, :, i * 512 - 3:i * 512 + 512], reads=[b_graw[i - 1], b_graw[i]], writes=[r_b])
                ts("pool", c_t, r_t[:, 3:515], convT[0][:, gi, 3:4], None, ALU.mult, None, [r_b, convT[1]], [c_b])
                for j_ in (2, 1, 0):
                    stt(c_t, r_t[:, j_:j_ + 512], convT[0][:, gi, j_:j_ + 1], c_t, ALU.mult, ALU.add, [r_b, convT[1], c_b], [c_b])
                if gi >= 4:
                    act(qkvn[gi][0][:, cols], c_t, AF.Silu, [c_b], [b_qkvn[gi][i]])
                    continue
                act(s_t, c_t, AF.Silu, [c_b], [s_b])
                q_t, q_b = sqb[gi % 2]
                tt("pool", q_t, s_t, s_t, ALU.mult, [s_b], [q_b])
                ps, pb = psum[gi % 4]
                mm(ps[:, :], ones_b[0], q_t, True, True, [ones_b[1], q_b], [pb])
                rs_t, rs_b = rst[gi % 2]
                act(rs_t, ps[:, :], AF.Ln, [pb, b_epsc], [rs_b], bias=epsc[:, 0:1])
                act(rs_t, rs_t, AF.Exp, [rs_b], [rs_b], scale=-0.5)
                if gi < 2:
                    stt(qkvn[gi][0][:, cols], s_t, 128.0 ** -0.5, rs_t, ALU.mult, ALU.mult, [s_b, rs_b], [b_qkvn[gi][i]])
                else:
                    tt("dve", qkvn[gi][0][:, cols], s_t, rs_t, ALU.mult, [s_b, rs_b], [b_qkvn[gi][i]])
        gcols = ar.tile("gcols", [NTT, 2], F32)
        beta = ar.tile("beta", [NTT, 2], F32)
        nbeta = ar.tile("nbeta", [NTT, 2], F32)
        nA = ar.tile("nA", [2], F32)
        act(nA[0], alog[0], AF.Exp, [alog[1]], [nA[1]])
        ts("dve", nA[0], nA[0], -1.0, None, ALU.mult, None, [nA[1]], [nA[1]])
        for h in range(2):
            act(gcols[0][:, :, h], ab_tm[0][:, :, h], AF.Exp, b_ab + [dtb[1]], [gcols[1]], bias=dtb[0][:, h:h + 1])
        act(gcols[0], gcols[0], AF.Ln, [gcols[1], onec[1]], [gcols[1]], bias=onec[0][:, 0:1])
        for h in range(2):
            ts("dve", gcols[0][:, :, h], gcols[0][:, :, h], nA[0][:, h:h + 1], None, ALU.mult, None, [gcols[1], nA[1]], [gcols[1]])
        act(beta[0], ab_tm[0][:, :, 2:4], AF.Exp, b_ab, [beta[1]], scale=-1.0)
        ts("dve", beta[0], beta[0], 1.0, None, ALU.add, None, [beta[1]], [beta[1]])
        S.op("dve", lambda e: e.reciprocal(out=beta[0], in_=beta[0]), reads=[beta[1]], writes=[beta[1]])
        ts("dve", nbeta[0], beta[0], -1.0, None, ALU.mult, None, [beta[1]], [nbeta[1]])
        gflat = gcols[0].rearrange("p a b -> p (a b)")
        bflat = beta[0].rearrange("p a b -> p (a b)")
        nbflat = nbeta[0].rearrange("p a b -> p (a b)")
        eg = ar.tile("eg", [N2], F32)
        ekd = ar.tile("ekd", [N2], F32)
        et0 = ar.tile("et0", [N2], F32)
        et1 = ar.tile("et1", [N2], F32)
        bk = ar.tile("bk", [N2], F32)
        ps, pb = psum[0]
        mm(ps[:, 0:N2], cn["tribd"][0], gflat, True, True, [cn["tribd"][1], gcols[1]], [pb])
        ps1, pb1 = psum[1]
        mm(ps1[:, 0:N2], cn["blockones"][0], gflat, True, True, [cn["blockones"][1], gcols[1]], [pb1])
        ps2, pb2 = psum[2]
        mm(ps2[:, 0:N2], cn["half0"][0], gflat, True, True, [cn["half0"][1], gcols[1]], [pb2])
        ps3, pb3 = psum[3]
        mm(ps3[:, 0:N2], cn["half1"][0], gflat, True, True, [cn["half1"][1], gcols[1]], [pb3])
        act(eg[0], ps[:, 0:N2], AF.Exp, [pb], [eg[1]])
        cp("dve", bk[0], ps[:, 0:N2], [pb], [bk[1]])
        tt("dve", ekd[0], ps1[:, 0:N2], bk[0], ALU.subtract, [pb1, bk[1]], [ekd[1]])
        act(ekd[0], ekd[0], AF.Exp, [ekd[1]], [ekd[1]])
        act(et0[0], ps2[:, 0:N2], AF.Exp, [pb2], [et0[1]])
        act(et1[0], ps3[:, 0:N2], AF.Exp, [pb3], [et1[1]])
        tt("dve", bk[0], bflat, eg[0], ALU.mult, [beta[1], eg[1], bk[1]], [bk[1]])
        et = [et0, et1]
        S.barrier()
        Sst = [ar.tile("Sst%d" % h, [128], F32) for h in range(2)]
        Sbf = [ar.tile("Sbf%d" % h, [128], BF16) for h in range(2)]
        for h in range(2):
            memset("pool", Sst[h][0], 0.0, [Sst[h][1]])
            memset("pool", Sbf[h][0], 0.0, [Sbf[h][1]])
        W = []
        for h in range(2):
            w_ = {}
            w_["R"] = [ar.tile("R%d_%d" % (h, i), [256], F32) for i in range(2)]
            w_["GS"] = ar.tile("GS%d" % h, [128], F32)
            w_["EE"] = ar.tile("EE%d" % h, [256], F32)
            w_["X"] = ar.tile("X%d" % h, [256], F32)
            w_["PP"] = [ar.tile("PP%d_%d" % (h, i), [256], F32) for i in range(2)]
            w_["QKm"] = ar.tile("QKm%d" % h, [128], BF16)
            w_["kdec"] = ar.tile("kdec%d" % h, [128], BF16)
            w_["vnew"] = ar.tile("vnew%d" % h, [128], BF16)
            w_["wT"] = ar.tile("wT%d" % h, [128], F32)
            w_["otm"] = ar.tile("otm%d" % h, [128], F32)
            w_["oqk"] = ar.tile("oqk%d" % h, [128], F32)
            w_["zt"] = ar.tile("zt%d" % h, [128], BF16)
            w_["zs"] = ar.tile("zs%d" % h, [128], F32)
            w_["ss"] = ar.tile("ss%d" % h, [2], F32)
            w_["ybf"] = ar.tile("ybf%d" % h, [128], BF16)
            W.append(w_)
        sqj = ar.tile("sqj", [128], F32)
        ybst = [ar.tile("ybst%d" % i, [2, 512], BF16) for i in range(2)]
        b_yb = S.buf("yb_scr")

        def gdn_steps(sc, h):
            w_ = W[h]
            col = sc * 2 + h
            tcols = slice(sc * 128, (sc + 1) * 128)
            ti = sc // 4
            bank = [psum[4 * h + i] for i in range(4)]
            QTn, KTn, VTn = qkvn[0 + h][0][:, tcols], qkvn[2 + h][0][:, tcols], qkvn[4 + h][0][:, tcols]
            bq, bk_, bv = b_qkvn[0 + h][ti], b_qkvn[2 + h][ti], b_qkvn[4 + h][ti]
            pA, pAb = bank[0]
            pAv = pA.bitcast(BF16)
            tr(pAv[:, 0:128], KTn, ident_b, [bk_, b_ident_b], [pAb])
            tr(pAv[:, 128:256], VTn, ident_b, [bv, b_ident_b], [pAb])
            R0, R0b = w_["R"][0]
            ts("dve", R0[:, 0:128], pAv[:, 128:256], bflat[:, col:col + 1], None, ALU.mult, None, [pAb, beta[1]], [R0b])
            ts("dve", R0[:, 128:256], pAv[:, 0:128], bk[0][:, col:col + 1], None, ALU.mult, None, [pAb, bk[1]], [R0b])
            act(w_["kdec"][0], pAv[:, 0:128], AF.Copy, [pAb, ekd[1]], [w_["kdec"][1]], scale=ekd[0][:, col:col + 1])
            yield
            pB, pBb = bank[1]
            mm(pB[:, 0:128], KTn, KTn, True, True, [bk_], [pBb])
            mm(pB[:, 128:256], KTn, QTn, False, True, [bk_, bq], [pBb])
            GS, GSb = w_["GS"]
            ts("pool", GS, cn["strictbd"][0], gflat[:, col:col + 1], None, ALU.mult, None, [cn["strictbd"][1], gcols[1]], [GSb])
            pC, pCb = bank[2]
            mm(pC[:, 0:128], ident_f, cn["negL"][0], True, False, [b_ident_f, cn["negL"][1]], [pCb])
            mm(pC[:, 0:128], cn["tribd"][0], GS, False, True, [cn["tribd"][1], GSb], [pCb])
            mm(pC[:, 128:256], ident_f, cn["negU"][0], False, False, [b_ident_f, cn["negU"][1]], [pCb])
            mm(pC[:, 128:256], GS, cn["tribd"][0], False, True, [cn["tribd"][1], GSb], [pCb])
            EE, EEb = w_["EE"]
            act(EE, pC[:, 0:256], AF.Exp, [pCb], [EEb])
            yield
            X, Xb = w_["X"]
            stt(X[:, 0:128], pB[:, 0:128], nbflat[:, col:col + 1], EE[:, 0:128], ALU.mult, ALU.mult, [pBb, nbeta[1], EEb], [Xb])
            tt("dve", w_["QKm"][0], pB[:, 128:256], EE[:, 128:256], ALU.mult, [pBb, EEb], [w_["QKm"][1]])
            pD, pDb = bank[3]
            tr(pD[:, 0:128], X[:, 0:128], ident_f, [Xb, b_ident_f], [pDb])
            cp("act", X[:, 128:256], pD[:, 0:128], [pDb], [Xb])
            yield
            P, Pb = X, Xb
            Rc, Rcb = R0, R0b
            for lev in range(6):
                pR, pRb = bank[0]
                mm(pR[:, 0:256], P[:, 128:256], Rc, True, True, [Pb, Rcb], [pRb])
                Rn, Rnb = w_["R"][(lev + 1) % 2]
                tt("dve", Rn, pR[:, 0:256], Rc, ALU.add, [pRb, Rcb], [Rnb])
                Rc, Rcb = Rn, Rnb
                if lev < 5:
                    pP, pPb = bank[1 + lev % 2]
                    mm(pP[:, 0:128], P[:, 128:256], P[:, 0:128], True, True, [Pb], [pPb])
                    mm(pP[:, 128:256], P[:, 0:128], P[:, 128:256], False, True, [Pb], [pPb])
                    Pn, Pnb = w_["PP"][lev % 2]
                    cp("act", Pn, pP[:, 0:256], [pPb], [Pnb])
                    P, Pb = Pn, Pnb
                yield
            pD, pDb = bank[3]
            tr(pD[:, 0:128], Rc[:, 128:256], ident_f, [Rcb, b_ident_f], [pDb])
            cp("act", w_["wT"][0], pD[:, 0:128], [pDb], [w_["wT"][1]])
            dma("sp", w_["zt"][0], z_scr[tcols, h * 128:(h + 1) * 128], reads=[b_zscr[ti]], writes=[w_["zt"][1]])
            yield
            S_, S_b = Sst[h]
            Sb_, Sb_b = Sbf[h]
            for hf in range(2):
                pc = slice(64 * hf, 64 * hf + 64)
                p1, p1b = bank[1]
                mm(p1[pc, 0:128], w_["wT"][0][:, pc], S_, True, True, [w_["wT"][1], S_b], [p1b])
                tt("dve", w_["vnew"][0][pc, :], Rc[pc, 0:128], p1[pc, 0:128], ALU.subtract, [Rcb, p1b], [w_["vnew"][1]])
                pO, pOb = bank[2]
                mm(pO[pc, 0:128], QTn[:, pc], Sb_, True, True, [bq, Sb_b], [pOb])
                mm(pO[pc, 128:256], w_["QKm"][0][pc, pc], w_["vnew"][0][pc, :], False, True, [w_["QKm"][1], w_["vnew"][1]], [pOb])
                p2, p2b = bank[3]
                mm(p2[:, 0:128], w_["kdec"][0][pc, :], w_["vnew"][0][pc, :], True, True, [w_["kdec"][1], w_["vnew"][1]], [p2b])
                stt(S_, S_, et[hf][0][:, col:col + 1], p2[:, 0:128], ALU.mult, ALU.add, [S_b, et[hf][1], p2b], [S_b])
                cp("act", Sb_, S_, [S_b], [Sb_b])
                cp("act", w_["oqk"][0][pc, :], pO[pc, 128:256], [pOb], [w_["oqk"][1]])
                stt(w_["otm"][0][pc, :], pO[pc, 0:128], eg[0][pc, col:col + 1], w_["oqk"][0][pc, :], ALU.mult, ALU.add,
                    [pOb, eg[1], w_["oqk"][1]], [w_["otm"][1]])
                yield
            o_t, o_b = w_["otm"]
            ss_t, ss_b = w_["ss"]
            act(sqj[0], o_t, AF.Square, [o_b], [sqj[1], ss_b], accum_out=ss_t[:, 0:1])
            act(ss_t[:, 1:2], ss_t[:, 0:1], AF.Ln, [ss_b, b_epsc], [ss_b], bias=epsc[:, 0:1], scale=1.0 / 128)
            act(ss_t[:, 1:2], ss_t[:, 1:2], AF.Exp, [ss_b], [ss_b], scale=-0.5)
            z_t, z_b = w_["zt"]
            zs_t, zs_b = w_["zs"]
            act(zs_t, z_t, AF.Exp, [z_b], [zs_b], scale=-1.0)
            ts("pool", zs_t, zs_t, 1.0, None, ALU.add, None, [zs_b], [zs_b])
            S.op("dve", lambda e: e.reciprocal(out=zs_t, in_=zs_t), reads=[zs_b], writes=[zs_b])
            tt("pool", zs_t, zs_t, z_t, ALU.mult, [zs_b, z_b], [zs_b])
            stt(o_t, o_t, ss_t[:, 1:2], gnormbc[0], ALU.mult, ALU.mult, [o_b, ss_b, gnormbc[1]], [o_b])
            tt("dve", w_["ybf"][0], o_t, zs_t, ALU.mult, [o_b, zs_b], [w_["ybf"][1]])
            pY, pYb = bank[0]
            pYv = pY.bitcast(BF16)
            tr(pYv[:, 0:128], w_["ybf"][0], ident_b, [w_["ybf"][1], b_ident_b], [pYb])
            y_s, y_sb = ybst[(sc // 4) % 2]
            cp("act", y_s[:, h, (sc % 4) * 128:(sc % 4 + 1) * 128], pYv[:, 0:128], [pYb], [y_sb])
            yield

        for sc in range(NTT):
            gens = [gdn_steps(sc, 0), gdn_steps(sc, 1)]
            alive = [True, True]
            while any(alive):
                for h in range(2):
                    if alive[h]:
                        try:
                            next(gens[h])
                        except StopIteration:
                            alive[h] = False
            if sc % 4 == 3:
                y_s, y_sb = ybst[(sc // 4) % 2]
                c_ = (sc // 4) * 512
                dma("sp", yb_scr.rearrange("(a p) t -> p a t", p=128)[:, :, c_:c_ + 512], y_s, reads=[y_sb], writes=[b_yb])
        ar.release(mk)
        S.barrier()
        if "G" in dbg:
            o = dout("dbg_yb", [256, T], BF16)
            final_ops.append(dma("sp", o, yb_scr[:, :], reads=[b_yb], writes=[S.buf("x4")]))

    b_ya = locals().get("b_ya") or S.buf("ya_scr")
    b_yb = locals().get("b_yb") or S.buf("yb_scr")
    if "YOUT" in dbg:
        o = dout("ya_out", [256, T], BF16)
        final_ops.append(dma("sp", o, ya_scr[:, :], reads=[b_ya], writes=[S.buf("x7")]))
        o = dout("yb_out", [256, T], BF16)
        final_ops.append(dma("sp", o, yb_scr[:, :], reads=[b_yb], writes=[S.buf("x8")]))
    b_yaall, b_yball = S.buf("ya_all"), S.buf("yb_all")
    if "X" in phases:
        groups = [[0, 1, 2, 3], [4, 5, 6, 7]]
        S.op("pool", lambda e: e.collective_compute("AllGather", ALU.bypass, replica_groups=groups,
                                                    ins=[ya_scr.ap().opt()], outs=[ya_all.ap().opt()]),
             reads=[b_ya], writes=[b_yaall], cc=True)
        S.op("pool", lambda e: e.collective_compute("AllGather", ALU.bypass, replica_groups=groups,
                                                    ins=[yb_scr.ap().opt()], outs=[yb_all.ap().opt()]),
             reads=[b_yb], writes=[b_yball], cc=True)

        if "X" in dbg:
            o = dout("dbg_yaall", [1024, T], BF16)
            final_ops.append(dma("sp", o, ya_all[:, :], reads=[b_yaall], writes=[S.buf("x5")]))
            o = dout("dbg_yball", [1024, T], BF16)
            final_ops.append(dma("sp", o, yb_all[:, :], reads=[b_yball], writes=[S.buf("x6")]))

    if "B" in phases:
        mk = ar.mark()
        MT = min(1024, TB)
        NMT = TB // MT
        WMAX = HALO + MT
        offt = ar.tile("offt", [2], I32)
        dyn = {}
        if not ystatic:
            dma("sp", offt[0][0:1, 0:2], tokoff_d, writes=[offt[1]])

        def load_off(e):
            ins = None
            for nm, ci in (("v", 0), ("vh", 1)):
                ctx = e.register("tokoff_reg_" + nm)
                reg = ctx.__enter__()
                ins = e.reg_load(reg, offt[0][0:1, ci:ci + 1])
                dyn[nm] = e.snap(reg)
            return ins
        if not ystatic:
            S.op("sp", load_off, reads=[offt[1]], writes=[S.buf("regdummy")])
        hmask = ar.tile("hmask", [1], F32)
        dma("sp", hmask[0], halomask_d, writes=[hmask[1]])
        fconvT = ar.tile("fconvT", [44, 3], F32)
        dma("sp", fconvT[0], fconvT_d.rearrange("p (a b) -> p a b", a=44), writes=[fconvT[1]])
        fconvb = ar.tile("fconvb", [44], F32)
        dma("sp", fconvb[0], fconvbT_d, writes=[fconvb[1]])
        nfbc = ar.tile("nfbc", [D], F32)
        dma("sp", nfbc[0], nfbc_d, writes=[nfbc[1]])
        ones_b2 = ar.tile("ones_b2", [128], BF16)
        dma("pool", ones_b2[0], cst_d["ones"], writes=[ones_b2[1]])
        carry = ar.tile("carry", [44, 2], F32)
        memset("pool", carry[0], 0.0, [carry[1]])
        xT = ar.tile("xT", [8, WMAX], F32)
        rstd2 = ar.tile("rstd2", [WMAX], F32)
        ya_v = ya_all.ap().rearrange("(k p) t -> p k t", p=128)
        yb_v = yb_all.ap().rearrange("(k p) t -> p k t", p=128)
        nwl = [0]

        def wload(dst, dbuf, src):
            q = "pool"
            nwl[0] += 1
            dma(q, dst, src, writes=[dbuf])

        for mt in range(NMT):
            c0 = 0 if mt == 0 else HALO + mt * MT
            c1 = HALO + (mt + 1) * MT
            ncol = c1 - c0
            sbs = []
            lc = 0
            if mt == 0:
                sbs.append((0, HALO))
                lc = HALO
            while lc < ncol:
                sbs.append((lc, min(512, ncol - lc)))
                lc += 512
            mkA = ar.mark()
            hTb = ar.tile("hTb", [8, WMAX], BF16)
            yaT = ar.tile("yaT", [8, WMAX], BF16)
            ybT = ar.tile("ybT", [8, WMAX], BF16)
            mixed = ar.tile("mixed", [8, WMAX], BF16)
            xld = [ar.tile("xld%d" % i, [D], F32) for i in range(2)]
            xnb = [ar.tile("xnb%d" % i, [D], BF16) for i in range(2)]
            sqj2 = ar.tile("sqj2", [D], BF16)
            ssb = [ar.tile("ssb%d" % i, [2], F32) for i in range(2)]
            subt = []
            r0 = 0
            while r0 < ncol:
                subt.append((r0, c0 + r0, 128))
                r0 += 128
            for si, (lc_, row, nr) in enumerate(subt):
                x_t, x_b = xld[si % 2]
                xn_t, xn_b = xnb[si % 2]
                ss_t, ss_b = ssb[si % 2]
                dma("sp", x_t[0:nr, :], xown[row:row + nr, :], writes=[x_b])
                act(sqj2[0][0:nr, :], x_t[0:nr, :], AF.Square, [x_b], [sqj2[1], ss_b], accum_out=ss_t[0:nr, 0:1])
                act(ss_t[0:nr, 1:2], ss_t[0:nr, 0:1], AF.Ln, [ss_b, b_epsc], [ss_b], bias=epsc[0:nr, 0:1], scale=1.0 / D)
                act(ss_t[0:nr, 1:2], ss_t[0:nr, 1:2], AF.Exp, [ss_b], [ss_b], scale=-0.5)
                act(xn_t[0:nr, :], x_t[0:nr, :], AF.Copy, [x_b, ss_b], [xn_b], scale=ss_t[0:nr, 1:2])
                pt, ptb = psum[si % 2]
                ptv = pt.bitcast(BF16)
                for k in range(8):
                    tr(ptv[:, k * 128:k * 128 + nr], xn_t[0:nr, k * 128:(k + 1) * 128], ident_b[0:nr, 0:nr], [xn_b, b_ident_b], [ptb])
                for k in range(8):
                    o_ = hTb[0][:, k, lc_:lc_ + nr]
                    i_ = ptv[:, k * 128:k * 128 + nr]
                    if k % 2 == 0:
                        act(o_, i_, AF.Identity, [ptb, b_amix, b_modT], [hTb[1]], bias=modT[:, k:k + 1], scale=amix[:, k:k + 1])
                    else:
                        ts("dve", o_, i_, amix[:, k:k + 1], modT[:, k:k + 1], ALU.mult, ALU.add, [ptb, b_amix, b_modT], [hTb[1]])
                for half in range(2):
                    pf, pfb = psum[2 + half]
                    for kk_ in range(4):
                        k = half * 4 + kk_
                        tr(pf[:, kk_ * 128:kk_ * 128 + nr], x_t[0:nr, k * 128:(k + 1) * 128], ident_f[0:nr, 0:nr], [x_b, b_ident_f], [pfb])
                    src = pf[:, :].rearrange("p (a q) -> p a q", a=4)[:, :, 0:nr]
                    if half == 0:
                        cp("dve", xT[0][:, 0:4, lc_:lc_ + nr], src, [pfb], [xT[1]])
                    else:
                        cp("act", xT[0][:, 4:8, lc_:lc_ + nr], src, [pfb], [xT[1]])
            def ld_y(dst, src_v):
                def fn(e):
                    if mt == 0:
                        return [e.dma_start(out=dst[:, :, 0:HALO], in_=src_v[:, :, bass.ds(dyn["vh"], HALO)]),
                                e.dma_start(out=dst[:, :, HALO:HALO + MT], in_=src_v[:, :, bass.ds(dyn["v"], MT)])]
                    return [e.dma_start(out=dst[:, :, 0:MT], in_=src_v[:, :, bass.ds(dyn["v"] + mt * MT, MT)])]
                return fn
            nd_ = 2 if mt == 0 else 1
            if ystatic:
                dma("sp", yaT[0][:, :, 0:ncol], ya_in.rearrange("(k p) t -> p k t", p=128)[:, :, c0:c0 + ncol], writes=[yaT[1]])
                dma("sp", ybT[0][:, :, 0:ncol], yb_in.rearrange("(k p) t -> p k t", p=128)[:, :, c0:c0 + ncol], writes=[ybT[1]])
            else:
                S.op("sp", ld_y(yaT[0], ya_v), reads=[b_yaall], writes=[yaT[1]], ndma=nd_)
                S.op("sp", ld_y(ybT[0], yb_v), reads=[b_yball], writes=[ybT[1]], ndma=nd_)
            wb3 = [[ar.tile("wb3_%d_%d" % (i, j), [8, 128], BF16) for j in range(4)] for i in range(2)]
            sg = [ar.tile("sg%d" % i, [512], F32) for i in range(2)]
            t12 = [ar.tile("t12_%d" % i, [512], F32) for i in range(2)]
            for oc in range(8):
                wset = wb3[oc % 2]
                wload(wset[0][0], wset[0][1], wpn_b[oc].rearrange("p (k c) -> p k c", k=8))
                wload(wset[1][0], wset[1][1], wpg_b[oc].rearrange("p (k c) -> p k c", k=8))
                wload(wset[2][0], wset[2][1], wmerge_b[oc].rearrange("p (k c) -> p k c", k=8))
                wload(wset[3][0], wset[3][1], wmerge_b[8 + oc].rearrange("p (k c) -> p k c", k=8))
                for (lc_, w_) in sbs:
                    cs = slice(lc_, lc_ + w_)
                    banks = [psum[4 + i] for i in range(4)]
                    srcs = [yaT, ybT, hTb, hTb]
                    for bi in range(4):
                        ps, pb = banks[bi]
                        for k in range(8):
                            mm(ps[:, 0:w_], wset[bi][0][:, k, :], srcs[bi][0][:, k, cs], k == 0, k == 7,
                               [wset[bi][1], srcs[bi][1]], [pb])
                    for bi in range(2):
                        act(sg[bi][0][:, 0:w_], banks[2 + bi][0][:, 0:w_], AF.Sigmoid, [banks[2 + bi][1]], [sg[bi][1]])
                        tt("dve", t12[bi][0][:, 0:w_], banks[bi][0][:, 0:w_], sg[bi][0][:, 0:w_], ALU.mult,
                           [banks[bi][1], sg[bi][1]], [t12[bi][1]])
                    tt("pool", mixed[0][:, oc, cs], t12[0][0][:, 0:w_], t12[1][0][:, 0:w_], ALU.add, [t12[0][1], t12[1][1]], [mixed[1]])
            wb4 = [ar.tile("wb4_%d" % i, [8, 128], BF16) for i in range(2)]
            for oc in range(8):
                w_t, w_b = wb4[oc % 2]
                wload(w_t, w_b, wout_b[oc].rearrange("p (k c) -> p k c", k=8))
                for si, (lc_, w_) in enumerate(sbs):
                    cs = slice(lc_, lc_ + w_)
                    ps, pb = psum[si % 2]
                    for k in range(8):
                        mm(ps[:, 0:w_], w_t[:, k, :], mixed[0][:, k, cs], k == 0, k == 7, [w_b, mixed[1]], [pb])
                    stt(xT[0][:, oc, cs], ps[:, 0:w_], modT[:, 16 + oc:17 + oc], xT[0][:, oc, cs], ALU.mult, ALU.add,
                        [pb, b_modT, xT[1]], [xT[1]])
            ar.release(mkA)
            S.barrier()
            mkB = ar.mark()
            hfT = ar.tile("hfT", [8, WMAX], BF16)
            actT = ar.tile("actT", [22, WMAX], BF16)
            mkC = ar.mark()
            sq5 = [ar.tile("sq5_%d" % i, [WMAX], BF16) for i in range(2)]
            tmp5 = [ar.tile("tmp5_%d" % i, [WMAX], F32) for i in range(2)]
            for oc in range(8):
                q_t, q_b = sq5[oc % 2]
                tt("pool", q_t[:, 0:ncol], xT[0][:, oc, 0:ncol], xT[0][:, oc, 0:ncol], ALU.mult, [xT[1]], [q_b])
                for si, (lc_, w_) in enumerate(sbs):
                    ps, pb = psum[si]
                    mm(ps[:, 0:w_], ones_b2[0], q_t[:, lc_:lc_ + w_], oc == 0, oc == 7, [ones_b2[1], q_b], [pb])
            for si, (lc_, w_) in enumerate(sbs):
                ps, pb = psum[si]
                act(rstd2[0][:, lc_:lc_ + w_], ps[:, 0:w_], AF.Ln, [pb, b_epsc], [rstd2[1]], bias=epsc[:, 0:1], scale=1.0 / D)
            act(rstd2[0][:, 0:ncol], rstd2[0][:, 0:ncol], AF.Exp, [rstd2[1]], [rstd2[1]], scale=-0.5)
            for oc in range(8):
                t_t, t_b = tmp5[oc % 2]
                tt("dve", t_t[:, 0:ncol], xT[0][:, oc, 0:ncol], rstd2[0][:, 0:ncol], ALU.mult, [xT[1], rstd2[1]], [t_b])
                act(hfT[0][:, oc, 0:ncol], t_t[:, 0:ncol], AF.Identity, [t_b, b_affn, b_modT], [hfT[1]], bias=modT[:, 24 + oc:25 + oc], scale=affn[:, oc:oc + 1])
            ar.release(mkC)
            S.barrier()
            mkC = ar.mark()
            wb6 = [[ar.tile("wb6_%d_%d" % (i, j), [8, 128], BF16) for j in range(2)] for i in range(2)]
            ub = [[ar.tile("ub_%d_%d" % (i, j), [2 + WMAX], F32) for j in range(2)] for i in range(2)]
            cc_ = [[ar.tile("cc_%d_%d" % (i, j), [WMAX], F32) for j in range(2)] for i in range(2)]
            ctmp = ar.tile("ctmp", [WMAX], F32)
            nb = 0
            for j in range(22):
                wset = wb6[j % 2]
                wload(wset[0][0], wset[0][1], wup_b[j].rearrange("p (k c) -> p k c", k=8))
                wload(wset[1][0], wset[1][1], wup_b[22 + j].rearrange("p (k c) -> p k c", k=8))
                for gv in range(2):
                    jj = j + 22 * gv
                    u_t, u_b = ub[j % 2][gv]
                    c_t, c_b = cc_[j % 2][gv]
                    cp("dve", u_t[:, 0:2], carry[0][:, jj, :], [carry[1]], [u_b])
                    for (lc_, w_) in sbs:
                        ps, pb = psum[3 + nb % 4]
                        nb += 1
                        for k in range(8):
                            mm(ps[:, 0:w_], wset[gv][0][:, k, :], hfT[0][:, k, lc_:lc_ + w_], k == 0, k == 7, [wset[gv][1], hfT[1]], [pb])
                        if mt == 0 and lc_ == 0:
                            ts("dve", u_t[:, 2:2 + HALO], ps[:, 0:HALO], hmask[0][:, 0:1], None, ALU.mult, None, [pb, hmask[1]], [u_b])
                        else:
                            cp("act", u_t[:, 2 + lc_:2 + lc_ + w_], ps[:, 0:w_], [pb], [u_b])
                    cp("dve", carry[0][:, jj, :], u_t[:, ncol:ncol + 2], [u_b], [carry[1]])
                    n_ = ncol
                    if gv == 0:
                        ts("dve", c_t[:, 0:n_], u_t[:, 2:2 + n_], fconvT[0][:, jj, 2:3], fconvb[0][:, jj:jj + 1], ALU.mult, ALU.add,
                           [u_b, fconvT[1], fconvb[1]], [c_b])
                        stt(c_t[:, 0:n_], u_t[:, 1:1 + n_], fconvT[0][:, jj, 1:2], c_t[:, 0:n_], ALU.mult, ALU.add, [u_b, fconvT[1], c_b], [c_b])
                        stt(c_t[:, 0:n_], u_t[:, 0:n_], fconvT[0][:, jj, 0:1], c_t[:, 0:n_], ALU.mult, ALU.add, [u_b, fconvT[1], c_b], [c_b])
                        act(c_t[:, 0:n_], c_t[:, 0:n_], AF.Silu, [c_b], [c_b])
                    else:
                        act(c_t[:, 0:n_], u_t[:, 2:2 + n_], AF.Identity, [u_b, fconvT[1], fconvb[1]], [c_b], bias=fconvb[0][:, jj:jj + 1], scale=fconvT[0][:, jj, 2:3])
                        ts("pool", ctmp[0][:, 0:n_], u_t[:, 1:1 + n_], fconvT[0][:, jj, 1:2], None, ALU.mult, None, [u_b, fconvT[1]], [ctmp[1]])
                        tt("pool", c_t[:, 0:n_], c_t[:, 0:n_], ctmp[0][:, 0:n_], ALU.add, [c_b, ctmp[1]], [c_b])
                        ts("pool", ctmp[0][:, 0:n_], u_t[:, 0:n_], fconvT[0][:, jj, 0:1], None, ALU.mult, None, [u_b, fconvT[1]], [ctmp[1]])
                        tt("pool", c_t[:, 0:n_], c_t[:, 0:n_], ctmp[0][:, 0:n_], ALU.add, [c_b, ctmp[1]], [c_b])
                tt("dve", actT[0][:, j, 0:ncol], cc_[j % 2][0][0][:, 0:ncol], cc_[j % 2][1][0][:, 0:ncol], ALU.mult,
                   [cc_[j % 2][0][1], cc_[j % 2][1][1]], [actT[1]])
            ar.release(mkC)
            S.barrier()
            wb7 = [ar.tile("wb7_%d" % i, [22, 128], BF16) for i in range(2)]
            for oc in range(8):
                w_t, w_b = wb7[oc % 2]
                wload(w_t, w_b, wdn_b[oc].rearrange("p (k c) -> p k c", k=22))
                for si, (lc_, w_) in enumerate(sbs):
                    cs = slice(lc_, lc_ + w_)
                    ps, pb = psum[si % 2]
                    for j in range(22):
                        mm(ps[:, 0:w_], w_t[:, j, :], actT[0][:, j, cs], j == 0, j == 21, [w_b, actT[1]], [pb])
                    stt(xT[0][:, oc, cs], ps[:, 0:w_], modT[:, 40 + oc:41 + oc], xT[0][:, oc, cs], ALU.mult, ALU.add,
                        [pb, b_modT, xT[1]], [xT[1]])
            ost = [ar.tile("ost%d" % i, [D], F32) for i in range(2)]
            fs = [ar.tile("fs%d" % i, [4], F32) for i in range(2)]
            sqj3 = ar.tile("sqj3", [512], F32)
            first_own = HALO if mt == 0 else 0
            nsub8 = (ncol - first_own) // 128
            for s8 in range(nsub8):
                lc_ = first_own + s8 * 128
                tok0 = (c0 + lc_) - HALO
                pa, pab = psum[2 + (s8 % 2) * 2]
                pb_, pbb = psum[3 + (s8 % 2) * 2]
                for k in range(8):
                    pp, ppb = (pa, pab) if k < 4 else (pb_, pbb)
                    tr(pp[:, (k % 4) * 128:(k % 4 + 1) * 128], xT[0][:, k, lc_:lc_ + 128], ident_f, [xT[1], b_ident_f], [ppb])
                f_t, f_b = fs[s8 % 2]
                act(sqj3[0], pa[:, :], AF.Square, [pab], [sqj3[1], f_b], accum_out=f_t[:, 0:1])
                act(sqj3[0], pb_[:, :], AF.Square, [pbb], [sqj3[1], f_b], accum_out=f_t[:, 1:2])
                tt("dve", f_t[:, 2:3], f_t[:, 0:1], f_t[:, 1:2], ALU.add, [f_b], [f_b])
                act(f_t[:, 3:4], f_t[:, 2:3], AF.Ln, [f_b, b_epsc], [f_b], bias=epsc[:, 0:1], scale=1.0 / D)
                act(f_t[:, 3:4], f_t[:, 3:4], AF.Exp, [f_b], [f_b], scale=-0.5)
                o_t, o_b = ost[s8 % 2]
                stt(o_t[:, 0:512], pa[:, :], f_t[:, 3:4], nfbc[0][:, 0:512], ALU.mult, ALU.mult, [pab, f_b, nfbc[1]], [o_b])
                stt(o_t[:, 512:1024], pb_[:, :], f_t[:, 3:4], nfbc[0][:, 512:1024], ALU.mult, ALU.mult, [pbb, f_b, nfbc[1]], [o_b])
                final_ops.append(dma("sp", out_d[tok0:tok0 + 128, :], o_t, reads=[o_b], writes=[S.buf("out%d" % tok0)]))
            ar.release(mkB)
            S.barrier()
        ar.release(mk)
    cnts = S.emit(final_ops)
    st.close()
    return nc, cnts, ar.peak


_CACHE = {}


def _get(T, dbg, phases, ystatic=False):
    key = (T, tuple(dbg), tuple(phases), ystatic)
    if key not in _CACHE:
        _CACHE[key] = build(T, dbg=dbg, phases=phases, ystatic=ystatic)
    return _CACHE[key][0]


def _in_maps(inputs, T):
    cst = host_constants(T)
    in_maps = []
    for c in range(8):
        b, g = divmod(c, 4)
        m = host_core_inputs(inputs, b, g, T)
        for k, v in cst.items():
            m["c_" + k] = v
        in_maps.append(m)
    return in_maps


def run_full(inputs, T, dbg=(), phases=("A0", "A1", "A2", "A3", "G", "X", "B"), trace=False):
    nc = _get(T, dbg, phases)
    return run_bass_kernel_spmd(nc, _in_maps(inputs, T), core_ids=list(range(8)), trace=trace)


def y_static_inputs(ya_full, yb_full, g, T):
    TB = T // 4
    outs = []
    for y in (ya_full, yb_full):
        o = np.zeros((1024, HALO + TB), y.dtype)
        o[:, HALO:] = y[:, g * TB:(g + 1) * TB]
        if g > 0:
            o[:, :HALO] = y[:, g * TB - HALO:g * TB]
        outs.append(o)
    return outs


def run_two_launch(inputs, T, trace=False):
    maps = _in_maps(inputs, T)
    ncA = _get(T, ("YOUT",), ("A0", "A1", "A2", "A3", "G"))
    resA = run_bass_kernel_spmd(ncA, maps, core_ids=list(range(8)), trace=trace)
    ncB = _get(T, (), ("A0", "B"), ystatic=True)
    for b in range(2):
        ya = np.concatenate([np.asarray(resA.results[b * 4 + g]["ya_out"]) for g in range(4)], 0)
        yb = np.concatenate([np.asarray(resA.results[b * 4 + g]["yb_out"]) for g in range(4)], 0)
        for g in range(4):
            a_, b_ = y_static_inputs(ya, yb, g, T)
            maps[b * 4 + g]["ya_in"] = a_
            maps[b * 4 + g]["yb_in"] = b_
    resB = run_bass_kernel_spmd(ncB, maps, core_ids=list(range(8)), trace=trace)
    return resA, resB


FUSED = False


def kernel(**inputs):
    T = inputs["x"].shape[1]
    inputs = {k: np.asarray(v) for k, v in inputs.items()}
    if FUSED:
        res = run_full(inputs, T)
    else:
        _, res = run_two_launch(inputs, T)
    TB = T // 4
    out = np.empty((2, T, D), np.float32)
    for c in range(8):
        b, g = divmod(c, 4)
        out[b, g * TB:(g + 1) * TB] = res.results[c]["out"]
    return out
```
